# Optimizing a Trainium2 kernel written in Bass

```python
import jax, jax.numpy as jnp
from jax import lax
import numpy as np

D_MODEL = 1024
BATCH = 8
SEQ = 8192
DEPTH = 1

GLA_HEADS = 4
GLA_DK = 64
GLA_DV = 128
GLA_GATE_RANK = 16
GLA_TAU = 16.0
GLA_CHUNK = 64
SWA_HEADS = 8
SWA_KV_HEADS = 2
SWA_HD = 64
SWA_WINDOW = 128
SWA_BLOCK = 128
ROPE_THETA = 500000.0
ROPE_DIM = SWA_HD // 4
D_FF = 2816
CONV_WIDTH = 3
EPS = 1e-6
MAX_POS_OFFSET = 4096

GLA_QK = GLA_HEADS * GLA_DK
GLA_V = GLA_HEADS * GLA_DV
SWA_Q = SWA_HEADS * SWA_HD
SWA_KV = SWA_KV_HEADS * SWA_HD
MIX_WIDTH = GLA_V + SWA_Q
IN_SPLITS = (GLA_QK, GLA_QK, GLA_V, GLA_GATE_RANK, GLA_V, SWA_Q, SWA_KV, SWA_KV)
IN_WIDTH = GLA_QK * 2 + GLA_V * 2 + GLA_GATE_RANK + SWA_Q + SWA_KV * 2

kernel_name = "hymba_gla_swa_sink_convffn_sandwich"


def rmsnorm(x, w):
    xf = x.astype(jnp.float32)
    y = xf * lax.rsqrt(jnp.mean(xf * xf, axis=-1, keepdims=True) + EPS)
    return (y * w.astype(jnp.float32)).astype(x.dtype)


def partial_rotary(x, positions):
    half = ROPE_DIM // 2
    inv_freq = ROPE_THETA ** (-jnp.arange(half, dtype=jnp.float32) * (2.0 / ROPE_DIM))
    ang = positions.astype(jnp.float32)[..., None] * inv_freq
    cos = jnp.cos(ang)[:, :, None, :]
    sin = jnp.sin(ang)[:, :, None, :]
    xr = x[..., :ROPE_DIM].astype(jnp.float32)
    x1, x2 = xr[..., :half], xr[..., half:]
    rot = jnp.concatenate([x1 * cos - x2 * sin, x2 * cos + x1 * sin], axis=-1)
    return jnp.concatenate([rot.astype(x.dtype), x[..., ROPE_DIM:]], axis=-1)


def gla_chunked(q, k, v, log_a):
    B, T, H, dk = q.shape
    dv = v.shape[-1]
    C = GLA_CHUNK
    n = T // C

    def to_chunks(t):
        return t.reshape(B, n, C, H, t.shape[-1]).transpose(1, 0, 3, 2, 4).astype(jnp.float32)

    qc = to_chunks(q * (dk ** -0.5))
    kc, vc, gc = to_chunks(k), to_chunks(v), to_chunks(log_a)
    causal = jnp.tril(jnp.ones((C, C), dtype=bool))[:, :, None]

    def step(S, inp):
        qi, ki, vi, gi = inp
        b = jnp.cumsum(gi, axis=2)
        o_inter = jnp.einsum('bhcd,bhde->bhce', qi * jnp.exp(b), S)
        diff = b[:, :, :, None, :] - b[:, :, None, :, :]
        decay = jnp.exp(jnp.where(causal, diff, -jnp.inf))
        A = jnp.einsum('bhid,bhjd,bhijd->bhij', qi, ki, decay)
        o_intra = jnp.einsum('bhij,bhje->bhie', A, vi)
        b_last = b[:, :, -1:, :]
        S_new = jnp.exp(b_last[:, :, 0, :])[..., None] * S + jnp.einsum(
            'bhcd,bhce->bhde', ki * jnp.exp(b_last - b), vi)
        return S_new, o_inter + o_intra

    S0 = jnp.zeros((B, H, dk, dv), jnp.float32)
    _, o = lax.scan(step, S0, (qc, kc, vc, gc))
    return o.transpose(1, 0, 3, 2, 4).reshape(B, T, H, dv).astype(v.dtype)


def swa_sink_attention(q, k, v, sinks):
    B, T, Hq, hd = q.shape
    Hkv = k.shape[2]
    G = Hq // Hkv
    W = SWA_BLOCK
    n = T // W
    qb = q.reshape(B, n, W, Hkv, G, hd).astype(jnp.float32)

    def with_prev(t):
        tb = t.reshape(B, n, W, Hkv, hd).astype(jnp.float32)
        prev = jnp.pad(tb, ((0, 0), (1, 0), (0, 0), (0, 0), (0, 0)))[:, :-1]
        return jnp.concatenate([prev, tb], axis=2)

    kb, vb = with_prev(k), with_prev(v)
    s = jnp.einsum('bnqhgd,bnshd->bhgnqs', qb, kb) * (hd ** -0.5)
    blk = jnp.arange(n)[:, None, None]
    qpos = blk * W + jnp.arange(W)[None, :, None]
    kpos = (blk - 1) * W + jnp.arange(2 * W)[None, None, :]
    mask = (kpos <= qpos) & (kpos > qpos - SWA_WINDOW) & (kpos >= 0)
    s = jnp.where(mask, s, -jnp.inf)
    sink = sinks.astype(jnp.float32).reshape(Hkv, G)[None, :, :, None, None]
    m = jnp.maximum(s.max(axis=-1), sink)
    p = jnp.exp(s - m[..., None])
    denom = p.sum(axis=-1) + jnp.exp(sink - m)
    o = jnp.einsum('bhgnqs,bnshd->bnqhgd', p, vb) / denom.transpose(0, 3, 4, 1, 2)[..., None]
    return o.reshape(B, T, Hq * hd).astype(q.dtype)


def causal_depthwise_conv(h, w, b):
    T = h.shape[1]
    hp = jnp.pad(h, ((0, 0), (CONV_WIDTH - 1, 0), (0, 0)))
    out = b
    for j in range(CONV_WIDTH):
        out = out + w[j] * hp[:, j:j + T]
    return out


def setup_inputs(seed: int = 0) -> dict:
    key = jax.random.key(seed)
    ks = jax.random.split(key, 18)
    nrm = jax.random.normal
    f32 = jnp.float32
    x = nrm(ks[0], (BATCH, SEQ, D_MODEL), f32)
    offs = jax.random.randint(ks[1], (BATCH, 1), 0, MAX_POS_OFFSET, dtype=jnp.int32)
    positions = (offs + jnp.arange(SEQ, dtype=jnp.int32)[None, :]).astype(jnp.int32)
    gain = lambda k_, d: 1.0 + 0.05 * nrm(k_, (DEPTH, d), f32)
    return {
        "x": x,
        "positions": positions,
        "pre_mix_norm": gain(ks[2], D_MODEL),
        "w_in": nrm(ks[3], (DEPTH, D_MODEL, IN_WIDTH), f32) * D_MODEL ** -0.5,
        "gla_gate_up": nrm(ks[4], (DEPTH, GLA_GATE_RANK, GLA_QK), f32) * GLA_GATE_RANK ** -0.5,
        "gla_gate_bias": 0.1 * nrm(ks[5], (DEPTH, GLA_QK), f32),
        "gla_out_norm": gain(ks[6], GLA_DV),
        "swa_sinks": nrm(ks[7], (DEPTH, SWA_HEADS), f32),
        "w_out": nrm(ks[8], (DEPTH, MIX_WIDTH, D_MODEL), f32) * MIX_WIDTH ** -0.5,
        "post_mix_norm": gain(ks[9], D_MODEL),
        "pre_ffn_norm": gain(ks[10], D_MODEL),
        "w_up": nrm(ks[11], (DEPTH, D_MODEL, 2 * D_FF), f32) * D_MODEL ** -0.5,
        "conv_w": nrm(ks[12], (DEPTH, CONV_WIDTH, 2 * D_FF), f32) * CONV_WIDTH ** -0.5,
        "conv_b": 0.02 * nrm(ks[13], (DEPTH, 2 * D_FF), f32),
        "w_down": nrm(ks[14], (DEPTH, D_FF, D_MODEL), f32) * D_FF ** -0.5,
        "post_ffn_norm": gain(ks[15], D_MODEL),
    }


def reference(x, positions, pre_mix_norm, w_in, gla_gate_up, gla_gate_bias, gla_out_norm,
              swa_sinks, w_out, post_mix_norm, pre_ffn_norm, w_up, conv_w, conv_b, w_down,
              post_ffn_norm):
    B, T, _ = x.shape
    split_points = np.cumsum(IN_SPLITS)[:-1].tolist()
    for l in range(DEPTH):
        h = rmsnorm(x, pre_mix_norm[l])
        proj = h @ w_in[l]
        gq, gk, gv, glr, gg, sq, sk, sv = jnp.split(proj, split_points, axis=-1)

        gate_logits = (glr @ gla_gate_up[l] + gla_gate_bias[l]).astype(jnp.float32)
        log_a = (jax.nn.log_sigmoid(gate_logits) / GLA_TAU).reshape(B, T, GLA_HEADS, GLA_DK)
        o_gla = gla_chunked(gq.reshape(B, T, GLA_HEADS, GLA_DK),
                            gk.reshape(B, T, GLA_HEADS, GLA_DK),
                            gv.reshape(B, T, GLA_HEADS, GLA_DV), log_a)
        o_gla = rmsnorm(o_gla, gla_out_norm[l]) * jax.nn.silu(gg.reshape(B, T, GLA_HEADS, GLA_DV))
        o_gla = o_gla.reshape(B, T, GLA_V)

        q = partial_rotary(sq.reshape(B, T, SWA_HEADS, SWA_HD), positions)
        k = partial_rotary(sk.reshape(B, T, SWA_KV_HEADS, SWA_HD), positions)
        v = sv.reshape(B, T, SWA_KV_HEADS, SWA_HD)
        o_swa = swa_sink_attention(q, k, v, swa_sinks[l])

        mix = jnp.concatenate([o_gla, o_swa], axis=-1) @ w_out[l]
        x = x + rmsnorm(mix, post_mix_norm[l])

        h = rmsnorm(x, pre_ffn_norm[l])
        u = causal_depthwise_conv(h @ w_up[l], conv_w[l], conv_b[l])
        val, gate = jnp.split(u, 2, axis=-1)
        y = (jax.nn.gelu(gate, approximate=True) * val) @ w_down[l]
        x = x + rmsnorm(y, post_ffn_norm[l])
    return x
```

```python
import math
from contextlib import ExitStack

import numpy as np
import concourse.bass as bass
import concourse.mybir as mybir
from concourse.bass_utils import run_bass_kernel_spmd

F32 = mybir.dt.float32
BF16 = mybir.dt.bfloat16
I32 = mybir.dt.int32
AF = mybir.ActivationFunctionType
ALU = mybir.AluOpType

D = 1024
T_FULL = 8192
TS = 512
NSUB = TS // 128
DFF = 2816
NFC = DFF // 128
EPS = 1e-6
ROPE_THETA = 500000.0

C_GQ, C_GK, C_GG, C_GLR = 0, 256, 512, 1024
C_TM = 1040
C_GV, C_SQ, C_SK, C_SV = C_TM, C_TM + 512, C_TM + 1024, C_TM + 1280
NC1 = C_TM + 1408

SAME_ENGINE_SYNC = True


class K:
    def __init__(self, nc, es):
        self.nc = nc
        self.es = es
        self.eng = dict(pe=nc.tensor, act=nc.scalar, dve=nc.vector, pool=nc.gpsimd, sp=nc.sync)
        self.sems = {}
        self.cnt = {}
        for e in ("pe", "act", "dve", "pool"):
            self.sems[e] = es.enter_context(nc.semaphore("sem_" + e))
            self.cnt[e] = 0
        self.seen = {e: {} for e in self.eng}
        self.recs = {}
        self.untracked = set()
        self.n_inst = 0
        self.out_tokens = {}

    def dma_sem(self, key):
        if key not in self.sems:
            self.sems[key] = self.es.enter_context(self.nc.semaphore("sd_" + key))
            self.cnt[key] = 0
        return key

    @staticmethod
    def box(ap):
        t = ap.tensor
        name = t.name
        dims = tuple(ap.ap)
        space = str(ap.space)
        if "PSUM" in space.upper():
            return name, 0, 128, 0, 1 << 30
        if "SB" in space.upper():
            ps = 1
            for v in list(t.shape)[1:]:
                ps *= int(v)
            off = int(ap.offset)
            p0 = off // ps
            lo = off % ps
            npart = int(dims[0][1])
            hi = lo + sum((int(c) - 1) * abs(int(s)) for s, c in dims[1:]) + 1
            return name, p0, p0 + npart, lo, hi
        off = int(ap.offset)
        hi = off + sum((int(c) - 1) * abs(int(s)) for s, c in dims) + 1
        return name, 0, 1, off, hi

    def _deps(self, eng, reads, writes):
        needs = {}

        def need(tok):
            k, v = tok
            if needs.get(k, 0) < v:
                needs[k] = v

        rb = [self.box(a) for a in reads]
        wb = [self.box(a) for a in writes]
        for (name, p0, p1, lo, hi) in rb:
            if name in self.untracked:
                continue
            for r in self.recs.get(name, ()):
                if r[5] and r[0] < p1 and p0 < r[1] and r[2] < hi and lo < r[3]:
                    need(r[4])
        for (name, p0, p1, lo, hi) in wb:
            if name in self.untracked:
                continue
            for r in self.recs.get(name, ()):
                if r[0] < p1 and p0 < r[1] and r[2] < hi and lo < r[3]:
                    need(r[4])
        e = self.eng[eng]
        for k, v in needs.items():
            if k == eng and (eng == "pe" or not SAME_ENGINE_SYNC):
                continue
            if self.seen[eng].get(k, 0) >= v:
                continue
            e.wait_ge(self.sems[k], v)
            self.n_inst += 1
            self.seen[eng][k] = v
        return rb, wb

    def _record(self, tok, rb, wb):
        for (name, p0, p1, lo, hi) in wb:
            if name in self.untracked:
                continue
            lst = self.recs.setdefault(name, [])
            lst[:] = [r for r in lst if not (p0 <= r[0] and r[1] <= p1 and lo <= r[2] and r[3] <= hi)]
            lst.append((p0, p1, lo, hi, tok, True))
        for (name, p0, p1, lo, hi) in rb:
            if name in self.untracked:
                continue
            lst = self.recs.setdefault(name, [])
            lst[:] = [r for r in lst if not ((not r[5]) and r[4][0] == tok[0] and p0 <= r[0] and r[1] <= p1
                                             and lo <= r[2] and r[3] <= hi)]
            lst.append((p0, p1, lo, hi, tok, False))

    def op(self, eng, fn, reads, writes):
        rb, wb = self._deps(eng, reads, writes)
        inst = fn()
        self.cnt[eng] += 1
        inst.then_inc(self.sems[eng], 1)
        self.n_inst += 1
        self._record((eng, self.cnt[eng]), rb, wb)

    def dma(self, q, out, in_, semkey, is_output=False):
        self.dma_sem(semkey)
        rb, wb = self._deps(q, [in_], [out])
        inst = self.eng[q].dma_start(out=out, in_=in_)
        self.cnt[semkey] += 16
        inst.then_inc(self.sems[semkey], 16)
        self.n_inst += 1
        tok = (semkey, self.cnt[semkey])
        self._record(tok, rb, wb)
        if is_output:
            self.out_tokens[semkey] = (q, self.cnt[semkey])

    def barrier(self):
        for e in ("pe", "act", "dve", "pool", "sp"):
            for key, sem in self.sems.items():
                v = self.cnt[key]
                if v > 0 and self.seen[e].get(key, 0) < v:
                    if key == e and e == "pe":
                        continue
                    self.eng[e].wait_ge(sem, v)
                    self.seen[e][key] = v
                    self.n_inst += 1
        self.recs.clear()

    def finish(self):
        for k, (q, v) in self.out_tokens.items():
            self.eng["sp"].wait_ge(self.sems[k], v)


def build_nc(T=T_FULL, debug_x1=False):
    NS = T // TS
    NBLK = T // 128
    nc = bass.Bass("TRN2", target_bir_lowering=False)

    def din(name, shape, dt=F32):
        return nc.dram_tensor(name, list(shape), dt, kind="ExternalInput").ap()

    x_d = din("x", [T, D])
    pos_d = din("pos", [128, NBLK], I32)
    cst_d = din("cst", [128, 896])
    w1_d = din("w1", [128, 8, NC1])
    g1_d = din("g1", [128, 8])
    gup_d = din("gup", [16, 256])
    gbias_d = din("gbias", [128, 2])
    gnorm_d = din("gnorm", [128, 1])
    sink_d = din("sink", [128, 4])
    wout_d = din("wout", [128, 8, D])
    wpost1_d = din("wpost1", [128, D])
    g2_d = din("g2", [128, 8])
    wup_d = din("wup", [2 * NFC, 128, D])
    cw_d = din("cw", [128, 2 * NFC, 3])
    cb_d = din("cb", [128, 2 * NFC])
    wdown_d = din("wdown", [128, NFC, D])
    wpost2_d = din("wpost2", [128, D])
    out_d = nc.dram_tensor("out", [T, D], F32, kind="ExternalOutput").ap()
    x1_d = nc.dram_tensor("x1s", [T, D], F32,
                          kind="ExternalOutput" if debug_x1 else "Internal").ap()

    es = ExitStack()
    k = K(nc, es)
    k.untracked.update(["x", "pos", "cst", "w1", "g1", "gup", "gbias", "gnorm", "sink", "wout",
                        "wpost1", "g2", "wup", "cw", "cb", "wdown", "wpost2", "out"])

    def sb(name, shape, dt=F32):
        return es.enter_context(nc.sbuf_tensor(name, list(shape), dt))

    def mm(out, lhsT, rhs, start=True, stop=True):
        k.op("pe", lambda: nc.tensor.matmul(out, lhsT, rhs, start=start, stop=stop), [lhsT, rhs], [out])

    def tr(out, in_, ident):
        k.op("pe", lambda: nc.tensor.transpose(out, in_, ident), [in_, ident], [out])

    def act(out, in_, func, bias=None, scale=None, accum=None):
        kw = {}
        rd = [in_]
        wr = [out]
        if bias is not None:
            kw["bias"] = bias
            if not isinstance(bias, (int, float)):
                rd.append(bias)
        if scale is not None:
            kw["scale"] = scale
            if not isinstance(scale, (int, float)):
                rd.append(scale)
        if accum is not None:
            kw["accum_out"] = accum
            wr.append(accum)
        k.op("act", lambda: nc.scalar.activation(out=out, in_=in_, func=func, **kw), rd, wr)

    def tt(eng, out, in0, in1, op):
        e = k.eng[eng]
        k.op(eng, lambda: e.tensor_tensor(out=out, in0=in0, in1=in1, op=op), [in0, in1], [out])

    def ts(eng, out, in0, s1, op0, s2=None, op1=None):
        e = k.eng[eng]
        rd = [in0] + [s for s in (s1, s2) if s is not None and not isinstance(s, (int, float))]
        if op1 is None:
            k.op(eng, lambda: e.tensor_scalar(out=out, in0=in0, scalar1=s1, scalar2=None, op0=op0), rd, [out])
        else:
            k.op(eng, lambda: e.tensor_scalar(out=out, in0=in0, scalar1=s1, scalar2=s2, op0=op0, op1=op1),
                 rd, [out])

    def stt(out, in0, scalar, in1, op0, op1):
        rd = [in0, in1] + ([] if isinstance(scalar, (int, float)) else [scalar])
        k.op("dve", lambda: nc.vector.scalar_tensor_tensor(out=out, in0=in0, scalar=scalar, in1=in1,
                                                           op0=op0, op1=op1), rd, [out])

    def cp(eng, out, in_):
        if eng == "act":
            act(out, in_, AF.Copy)
        else:
            e = k.eng[eng]
            k.op(eng, lambda: e.tensor_copy(out=out, in_=in_), [in_], [out])

    def memset(eng, ap, val):
        e = k.eng[eng]
        k.op(eng, lambda: e.memset(ap, val), [], [ap])

    def rstd_from_ss(ss_ap, lnv_ap, rstd_ap, n):
        act(lnv_ap, ss_ap, AF.Ln, bias=eps_c[:, 0:1], scale=1.0 / n)
        act(rstd_ap, lnv_ap, AF.Exp, scale=-0.5)

    PS = [es.enter_context(nc.psum_tensor("ps%d" % i, [128, 512], F32)) for i in range(8)]
    P0bf = PS[0][:, :].bitcast(BF16)

    ident_bf = sb("ident_bf", [128, 128], BF16)
    ones_bf = sb("ones_bf", [128, 128], BF16)
    eps_c = sb("eps_c", [128, 1])
    one_c = sb("one_c", [128, 1])
    memset("dve", ones_bf[:, :], 1.0)
    memset("dve", eps_c[:, :], EPS)
    memset("dve", one_c[:, :], 1.0)

    wpost = sb("wpost", [128, D])
    k.dma("sp", wpost[:, :], wpost1_d[:, :], "wpost")

    with ExitStack() as es1:
        def sb1(name, shape, dt=F32):
            return es1.enter_context(nc.sbuf_tensor(name, list(shape), dt))

        cst = sb1("cst_sb", [128, 896])
        maskA = sb1("maskA", [128, 256], BF16)
        scanmask = cst[:, 384:896]
        k.dma("sp", cst[:, :], cst_d[:, :], "cst")
        cp("dve", ident_bf[:, :], cst[:, 0:128])
        cp("dve", maskA[:, :], cst[:, 128:384])

        w1 = sb1("w1_bf", [128, 8, NC1], BF16)
        wout = sb1("wout_bf", [128, 8, D], BF16)
        g1 = sb1("g1_sb", [128, 8])
        gup = sb1("gup_sb", [16, 256])
        negb = sb1("negb", [128, 2])
        gnorm = sb1("gnorm_sb", [128, 1])
        sinkexp = sb1("sinkexp", [128, 4])
        sinkfull = sb1("sinkfull", [128, 512])
        SC = sb1("sincos", [128, 16, NBLK])

        es1a = ExitStack()

        def sba(name, shape, dt=F32):
            return es1a.enter_context(nc.sbuf_tensor(name, list(shape), dt))

        stage = [sba("stage%d" % i, [128, NC1]) for i in range(2)]
        zeros_c = sba("zeros_c", [128, 128])
        posi = sba("posi", [128, NBLK], I32)
        posf = sba("posf", [128, NBLK])
        rt_y = sba("rt_y", [128, 16, NBLK])
        rt_n = sba("rt_n", [128, 16, NBLK], I32)
        rt_f = sba("rt_f", [128, 16, NBLK])
        rt_g = sba("rt_g", [128, 16, NBLK])
        memset("dve", zeros_c[:, :], 0.0)

        k.dma("sp", g1[:, :], g1_d[:, :], "s_g1")
        k.dma("sp", gup[:, :], gup_d[:, :], "s_gup")
        k.dma("sp", negb[:, :], gbias_d[:, :], "s_negb")
        k.dma("sp", gnorm[:, :], gnorm_d[:, :], "s_gnorm")
        k.dma("sp", sinkexp[:, :], sink_d[:, :], "s_sink")
        k.dma("sp", posi[:, :], pos_d[:, :], "s_pos")
        ts("dve", negb[:, :], negb[:, :], -1.0, ALU.mult)
        act(sinkexp[:, :], sinkexp[:, :], AF.Exp)
        for cc in range(4):
            ts("dve", sinkfull[:, cc * 128:(cc + 1) * 128], zeros_c[:, :], sinkexp[:, cc:cc + 1], ALU.add)

        cp("dve", posf[:, :], posi[:, :])
        for f in range(8):
            inv = ROPE_THETA ** (-(f * 2.0 / 16.0)) / (2.0 * math.pi)
            ts("dve", rt_y[:, f, :], posf[:, :], float(inv), ALU.mult)
            ts("dve", rt_y[:, 8 + f, :], posf[:, :], float(inv), ALU.mult, 0.25, ALU.add)
        cp("dve", rt_n[:, :, :], rt_y[:, :, :])
        cp("dve", rt_f[:, :, :], rt_n[:, :, :])
        tt("dve", rt_f[:, :, :], rt_y[:, :, :], rt_f[:, :, :], ALU.subtract)
        ts("dve", rt_g[:, :, :], rt_f[:, :, :], 0.5, ALU.is_gt)
        tt("dve", rt_f[:, :, :], rt_f[:, :, :], rt_g[:, :, :], ALU.subtract)
        ts("dve", rt_g[:, :, :], rt_f[:, :, :], -0.5, ALU.is_lt)
        tt("dve", rt_f[:, :, :], rt_f[:, :, :], rt_g[:, :, :], ALU.add)
        act(SC[:, :, :], rt_f[:, :, :], AF.Sin, scale=2.0 * math.pi * (1.0 - 1e-6))

        for kk in range(8):
            st = stage[kk % 2]
            k.dma("sp", st[:, :], w1_d[:, kk, :], "stage%d" % (kk % 2))
            if kk % 2 == 0:
                ts("dve", w1[:, kk, :], st[:, :], g1[:, kk:kk + 1], ALU.mult)
            else:
                act(w1[:, kk, :], st[:, :], AF.Identity, scale=g1[:, kk:kk + 1])
        for kk in range(0, 8, 2):
            st = stage[(kk // 2) % 2]
            k.dma("sp", st[:, 0:2048].rearrange("p (a d) -> p a d", a=2), wout_d[:, kk:kk + 2, :],
                  "stage%d" % ((kk // 2) % 2))
            if (kk // 2) % 2 == 0:
                cp("dve", wout[:, kk:kk + 2, :], st[:, 0:2048].rearrange("p (a d) -> p a d", a=2))
            else:
                cp("act", wout[:, kk:kk + 2, :], st[:, 0:2048].rearrange("p (a d) -> p a d", a=2))
        k.barrier()
        es1a.close()

        xa1 = [sb1("xa1_%d" % i, [128, D]) for i in range(2)]
        htok = [sb1("htok%d" % i, [128, D], BF16) for i in range(2)]
        junk = sb1("junk", [128, D], BF16)
        ssq = sb1("ssq", [128, 8])
        lnv = sb1("lnv", [128, 8])
        rstd = sb1("rstd", [128, 8])
        hT = sb1("hT", [128, 8, TS], BF16)
        glrT = sb1("glrT", [16, TS])
        sp_t = sb1("sp_t", [128, 2, TS])
        bc_t = sb1("bc_t", [128, 2, TS])
        Eb = sb1("Eb", [128, 2, TS])
        Enb = sb1("Enb", [128, 2, TS])
        eblast = [sb1("eblast%d" % i, [128, 2, NSUB]) for i in range(2)]
        qg = [sb1("qg%d" % i, [128, 2, TS], BF16) for i in range(2)]
        kg = [sb1("kg%d" % i, [128, 2, TS], BF16) for i in range(2)]
        sgT = [sb1("sgT%d" % i, [128, 4, TS], BF16) for i in range(2)]
        v_g = [sb1("v_g%d" % i, [128, NSUB, 512], BF16) for i in range(2)]
        qT_s = [sb1("qT_s%d" % i, [128, NSUB, 4, 128], BF16) for i in range(2)]
        qk32 = [sb1("qk32_%d" % i, [128, 12, 16]) for i in range(2)]
        gx = [sb1("gx%d" % i, [128, TS]) for i in range(4)]
        rtmp = [sb1("rtmp%d" % i, [128, 12, 8]) for i in range(4)]
        qtok = [sb1("qtok%d" % i, [128, 12, 64], BF16) for i in range(2)]
        kT_s = sb1("kT_s", [128, 8, 2, 128], BF16)
        v_s = sb1("v_s", [128, 8, 128], BF16)
        pT = [sb1("pT%d" % i, [128, 512], BF16) for i in range(4)]
        den_sb = sb1("den_sb", [128, 512])
        AT_sb = [sb1("AT_sb%d" % i, [128, 256], BF16) for i in range(2)]
        ktok = sb1("ktok", [128, 256], BF16)
        S_st = sb1("S_st", [128, 2, 128])
        S_tmp = sb1("S_tmp", [128, 2, 128])
        S_pad = sb1("S_pad", [128, 4, 128], BF16)
        sq_bf = sb1("sq_bf", [128, 512], BF16)
        gl_rs = sb1("gl_rs", [128, 512])
        gl_on = sb1("gl_on", [128, 512])
        mixT = sb1("mixT", [128, 8, TS], BF16)
        ytmp = sb1("ytmp", [128, D])
        x1buf = [sb1("x1buf%d" % i, [128, D]) for i in range(2)]

        memset("dve", S_st[:, :, :], 0.0)
        memset("dve", S_pad[:, :, :], 0.0)
        P6bf = PS[6][:, :].bitcast(BF16)
        fbank = [0]

        def nb():
            fbank[0] += 1
            return 1 + fbank[0] % 2

        def front(s):
            par = s % 2

            def An(j):
                r0 = s * TS + j * 128
                xin = xa1[j % 2]
                ht = htok[j % 2]
                k.dma("sp", xin[:, :], x_d[r0:r0 + 128, :], "xa1_%d" % (j % 2))
                act(junk[:, :], xin[:, :], AF.Square, accum=ssq[:, j:j + 1])
                rstd_from_ss(ssq[:, j:j + 1], lnv[:, j:j + 1], rstd[:, j:j + 1], D)
                ts("dve", ht[:, :], xin[:, :], rstd[:, j:j + 1], ALU.mult)

            def At_pe(j):
                ht = htok[j % 2]
                for kk in range(8):
                    tr(P0bf[:, kk * 128:(kk + 1) * 128], ht[:, kk * 128:(kk + 1) * 128], ident_bf[:, :])

            def At_ew(j):
                cp("dve", hT[:, :, j * 128:(j + 1) * 128],
                   P0bf[:, 0:1024].rearrange("p (a t) -> p a t", a=8))

            def fm_chunk(bank, col0, m):
                for kk in range(8):
                    mm(PS[bank][0:m, :], w1[:, kk, col0:col0 + m], hT[:, kk, :], start=(kk == 0), stop=(kk == 7))

            for j in range(NSUB):
                An(j)
                if j > 0:
                    At_pe(j - 1)
                yield
                if j > 0:
                    At_ew(j - 1)
                    yield
            At_pe(NSUB - 1)
            yield
            At_ew(NSUB - 1)
            b = nb()
            fm_chunk(b, C_GLR, 16)
            yield
            cp("act", glrT[:, :], PS[b][0:16, :])
            yield
            bl = [nb(), nb()]
            for c in range(2):
                mm(PS[bl[c]][:, :], gup[:, c * 128:(c + 1) * 128], glrT[:, :])
            yield
            for c in range(2):
                act(sp_t[:, c, :], PS[bl[c]][:, :], AF.Exp, bias=negb[:, c:c + 1], scale=-1.0)
            for c in range(2):
                act(sp_t[:, c, :], sp_t[:, c, :], AF.Ln, bias=one_c[:, 0:1], scale=1.0)
            for c in range(2):
                k.op("dve", lambda c=c: nc.vector.tensor_tensor_scan(
                    out=bc_t[:, c, :], data0=scanmask, data1=sp_t[:, c, :], initial=0.0,
                    op0=ALU.mult, op1=ALU.add), [scanmask, sp_t[:, c, :]], [bc_t[:, c, :]])
            bg = [nb(), nb()]
            for h in range(2):
                fm_chunk(bg[h], C_GG + h * 128, 128)
            yield
            for c in range(2):
                act(Eb[:, c, :], bc_t[:, c, :], AF.Exp, scale=-1.0 / 16.0)
                act(Enb[:, c, :], bc_t[:, c, :], AF.Exp, scale=1.0 / 16.0)
            cp("pool", eblast[par][:, :, :],
               Eb[:, :, :].rearrange("p c (j t) -> p c j t", t=128)[:, :, :, 127])
            for h in range(2):
                cp("dve", gx[h][:, :], PS[bg[h]][:, :])
            bg = [nb(), nb()]
            for h in range(2):
                fm_chunk(bg[h], C_GG + (2 + h) * 128, 128)
            yield
            for h in range(2):
                cp("dve", gx[2 + h][:, :], PS[bg[h]][:, :])
            bq = [nb(), nb()]
            for c in range(2):
                fm_chunk(bq[c], C_GQ + c * 128, 128)
            yield
            for c in range(2):
                stt(qg[par][:, c, :], PS[bq[c]][:, :], 0.125, Eb[:, c, :], ALU.mult, ALU.mult)
            for h in range(4):
                act(sgT[par][:, h, :], gx[h][:, :], AF.Silu)
            yield
            bk = [nb(), nb()]
            for c in range(2):
                fm_chunk(bk[c], C_GK + c * 128, 128)
            yield
            for c in range(2):
                tt("dve", kg[par][:, c, :], PS[bk[c]][:, :], Enb[:, c, :], ALU.mult)
            yield

            def Ct_pe(j):
                qflat = qtok[j % 2][:, :, :].rearrange("p h d -> p (h d)")
                for cc in range(6):
                    tr(P0bf[:, cc * 128:(cc + 1) * 128], qflat[:, cc * 128:(cc + 1) * 128], ident_bf[:, :])

            def Ct_ew(j):
                slot = (s * NSUB + j) % 8
                cp("dve", qT_s[par][:, j, :, :], P0bf[:, 0:512].rearrange("p (a t) -> p a t", a=4))
                cp("dve", kT_s[:, slot, :, :], P0bf[:, 512:768].rearrange("p (a t) -> p a t", a=2))

            for j in range(NSUB):
                gb = s * NSUB + j
                slot = gb % 8
                q32 = qk32[j % 2]
                qtk = qtok[j % 2]
                lhs = lambda kk: hT[:, kk, j * 128:(j + 1) * 128]
                b1, b2 = nb(), nb()
                for kk in range(8):
                    mm(PS[b1][:, :], lhs(kk), w1[:, kk, C_GV:C_GV + 512], kk == 0, kk == 7)
                for kk in range(8):
                    mm(PS[b2][:, :], lhs(kk), w1[:, kk, C_SQ:C_SQ + 512], kk == 0, kk == 7)
                yield
                cp("act", v_g[par][:, j, :], PS[b1][:, :])
                pq = PS[b2][:, :].rearrange("p (h d) -> p h d", d=64)
                cp("act", q32[:, 0:8, :], pq[:, :, 0:16])
                cp("act", qtk[:, 0:8, 16:64], pq[:, :, 16:64])
                if j > 0:
                    Ct_pe(j - 1)
                b3 = nb()
                for kk in range(8):
                    mm(PS[b3][:, 0:384], lhs(kk), w1[:, kk, C_SK:C_SK + 384], kk == 0, kk == 7)
                yield
                pk = PS[b3][:, 0:256].rearrange("p (h d) -> p h d", d=64)
                cp("act", q32[:, 8:12, :], pk[:, :, 0:16])
                cp("act", qtk[:, 8:12, 16:64], pk[:, :, 16:64])
                cp("act", v_s[:, slot, :], PS[b3][:, 256:384])
                if j > 0:
                    Ct_ew(j - 1)
                sinb = SC[:, 0:8, gb].unsqueeze(1).broadcast_to([128, 12, 8])
                cosb = SC[:, 8:16, gb].unsqueeze(1).broadcast_to([128, 12, 8])
                x1v = q32[:, :, 0:8]
                x2v = q32[:, :, 8:16]
                tt("dve", rtmp[0][:, :, :], x1v, cosb, ALU.mult)
                tt("dve", rtmp[1][:, :, :], x2v, sinb, ALU.mult)
                tt("dve", qtk[:, :, 0:8], rtmp[0][:, :, :], rtmp[1][:, :, :], ALU.subtract)
                tt("dve", rtmp[2][:, :, :], x2v, cosb, ALU.mult)
                tt("dve", rtmp[3][:, :, :], x1v, sinb, ALU.mult)
                tt("dve", qtk[:, :, 8:16], rtmp[2][:, :, :], rtmp[3][:, :, :], ALU.add)
                yield
            Ct_pe(NSUB - 1)
            yield
            Ct_ew(NSUB - 1)
            yield

        def back(s):
            par = s % 2

            def F_pe(j):
                tsl = slice(j * 128, (j + 1) * 128)
                fb = (5, 7)
                for hf in range(2):
                    for kk in range(8):
                        mm(PS[fb[hf]][:, :], mixT[:, kk, tsl], wout[:, kk, hf * 512:(hf + 1) * 512],
                           start=(kk == 0), stop=(kk == 7))

            def F_ew(j):
                r0 = s * TS + j * 128
                xb = x1buf[j % 2]
                fb = (5, 7)
                for hf in range(2):
                    act(junk[:, hf * 512:(hf + 1) * 512], PS[fb[hf]][:, :], AF.Square,
                        accum=ssq[:, 4 + hf:5 + hf])
                tt("dve", ssq[:, 6:7], ssq[:, 4:5], ssq[:, 5:6], ALU.add)
                rstd_from_ss(ssq[:, 6:7], lnv[:, 6:7], rstd[:, 6:7], D)
                for hf in range(2):
                    stt(ytmp[:, hf * 512:(hf + 1) * 512], PS[fb[hf]][:, :], rstd[:, 6:7],
                        wpost[:, hf * 512:(hf + 1) * 512], ALU.mult, ALU.mult)
                tt("dve", xb[:, :], ytmp[:, :], xb[:, :], ALU.add)
                k.dma("pool", x1_d[r0:r0 + 128, :], xb[:, :], "x1st%d" % (j % 2), is_output=debug_x1)

            for j in range(NSUB):
                gb = s * NSUB + j
                slot = gb % 8
                pslot = (gb - 1) % 8
                tsl = slice(j * 128, (j + 1) * 128)
                kbs = ([] if gb == 0 else [(0, pslot)]) + [(1, slot)]
                for m in range(2):
                    for (kb, sl) in kbs:
                        for g in range(2):
                            mm(PS[3 + kb][:, g * 256:(g + 1) * 256],
                               kT_s[m * 64:(m + 1) * 64, sl, g, :],
                               qT_s[par][m * 64:(m + 1) * 64, j, 2 * g:2 * g + 2, :].rearrange("p a t -> p (a t)"))
                    if m == 0:
                        for c in range(2):
                            tr(P6bf[:, c * 128:(c + 1) * 128], kg[par][:, c, tsl], ident_bf[:, :])
                    yield
                    for (kb, sl) in kbs:
                        pt = pT[m * 2 + kb]
                        act(pt[:, :], PS[3 + kb][:, :], AF.Exp, scale=0.125)
                        if kb == 1:
                            k.op("pool", lambda pt=pt: nc.gpsimd.affine_select(
                                out=pt[:, :], in_=pt[:, :], pattern=[[0, 4], [1, 128]],
                                compare_op=ALU.is_ge, fill=0.0, base=0, channel_multiplier=-1),
                                [pt[:, :]], [pt[:, :]])
                        else:
                            k.op("pool", lambda pt=pt: nc.gpsimd.affine_select(
                                out=pt[:, :], in_=pt[:, :], pattern=[[0, 4], [-1, 128]],
                                compare_op=ALU.is_gt, fill=0.0, base=0, channel_multiplier=1),
                                [pt[:, :]], [pt[:, :]])
                    if m == 0:
                        cp("act", ktok[:, :], P6bf[:, 0:256])
                    yield
                if j > 0:
                    F_pe(j - 1)
                    yield
                    F_ew(j - 1)
                    yield
                for c in range(2):
                    for m in range(2):
                        mm(PS[3 + m][:, c * 128:(c + 1) * 128], kg[par][m * 64:(m + 1) * 64, c, tsl],
                           qg[par][m * 64:(m + 1) * 64, c, tsl])
                yield
                for m in range(2):
                    tt("dve", AT_sb[m][:, :], PS[3 + m][:, 0:256], maskA[:, :], ALU.mult)
                yield
                for g in range(2):
                    for m in range(2):
                        for idx, (kb, sl) in enumerate(kbs):
                            mm(PS[5][m * 64:(m + 1) * 64, g * 256:(g + 1) * 256],
                               v_s[:, sl, g * 64:(g + 1) * 64], pT[m * 2 + kb][:, g * 256:(g + 1) * 256],
                               start=(idx == 0), stop=(idx == len(kbs) - 1))
                for g in range(2):
                    for m in range(2):
                        for idx, (kb, sl) in enumerate(kbs):
                            mm(PS[6][m * 64:(m + 1) * 64, g * 256:(g + 1) * 256],
                               ones_bf[:, 0:64], pT[m * 2 + kb][:, g * 256:(g + 1) * 256],
                               start=(idx == 0), stop=(idx == len(kbs) - 1))
                yield
                tt("dve", den_sb[:, :], PS[6][:, :], sinkfull[:, :], ALU.add)
                act(den_sb[:, :], den_sb[:, :], AF.Ln)
                act(den_sb[:, :], den_sb[:, :], AF.Exp, scale=-1.0)
                tt("dve", mixT[:, 4:8, tsl],
                   PS[5][:, :].rearrange("p (a t) -> p a t", a=4),
                   den_sb[:, :].rearrange("p (a t) -> p a t", a=4), ALU.mult)
                yield
                for h in range(4):
                    c, m = h // 2, h % 2
                    mm(PS[7][:, h * 128:(h + 1) * 128], v_g[par][:, j, h * 128:(h + 1) * 128],
                       AT_sb[m][:, c * 128:(c + 1) * 128], start=True, stop=False)
                    mm(PS[7][:, h * 128:(h + 1) * 128], S_pad[:, h, :], qg[par][:, c, tsl], start=False, stop=True)
                for c in range(2):
                    for m in range(2):
                        h = 2 * c + m
                        mm(PS[3][m * 64:(m + 1) * 64, c * 128:(c + 1) * 128],
                           ktok[:, c * 128 + m * 64:c * 128 + (m + 1) * 64], v_g[par][:, j, h * 128:(h + 1) * 128])
                yield
                act(sq_bf[:, :], PS[7][:, :], AF.Square)
                tt("dve", S_tmp[:, :, :], PS[3][:, 0:256].rearrange("p (c e) -> p c e", c=2), S_st[:, :, :],
                   ALU.add)
                for c in range(2):
                    ts("dve", S_st[:, c, :], S_tmp[:, c, :], eblast[par][:, c, j:j + 1], ALU.mult)
                    for m in range(2):
                        cp("pool", S_pad[m * 64:(m + 1) * 64, 2 * c + m, :], S_st[m * 64:(m + 1) * 64, c, :])
                yield
                mm(PS[6][:, :], ones_bf[:, :], sq_bf[:, :])
                yield
                act(gl_rs[:, :], PS[6][:, :], AF.Ln, bias=eps_c[:, 0:1], scale=1.0 / 128.0)
                act(gl_rs[:, :], gl_rs[:, :], AF.Exp, scale=-0.5)
                stt(gl_on[:, :], PS[7][:, :], gnorm[:, 0:1], gl_rs[:, :], ALU.mult, ALU.mult)
                tt("pool", mixT[:, 0:4, tsl], gl_on[:, :].rearrange("p (a t) -> p a t", a=4),
                   sgT[par][:, :, tsl], ALU.mult)
                r0 = s * TS + j * 128
                k.dma("sp", x1buf[j % 2][:, :], x_d[r0:r0 + 128, :], "x1st%d" % (j % 2))
                yield
            F_pe(NSUB - 1)
            yield
            F_ew(NSUB - 1)
            yield

        def drain(g):
            for _ in g:
                pass

        def interleave(fg, bg, nf, nbk):
            fdone = 0
            for ib in range(nbk):
                try:
                    next(bg)
                except StopIteration:
                    break
                want = ((ib + 1) * nf) // nbk
                while fdone < want:
                    try:
                        next(fg)
                    except StopIteration:
                        fdone = nf
                        break
                    fdone += 1
            drain(bg)
            drain(fg)

        NF_PIECES = (2 * NSUB) + 9 + 3 * NSUB + 2
        NB_PIECES = 14 * NSUB
        drain(front(0))
        for s in range(NS):
            if s + 1 < NS:
                interleave(front(s + 1), back(s), NF_PIECES, NB_PIECES)
            else:
                drain(back(s))

    k.barrier()
    with ExitStack() as es2:
        def sb2(name, shape, dt=F32):
            return es2.enter_context(nc.sbuf_tensor(name, list(shape), dt))

        wup = sb2("wup_bf", [128, 8, 2 * DFF], BF16)
        wdown = sb2("wdown_bf", [128, NFC, D], BF16)
        g2 = sb2("g2_sb", [128, 8])
        cw = sb2("cw_sb", [128, 2 * NFC, 3])
        cb = sb2("cb_sb", [128, 2 * NFC])
        k.dma("sp", g2[:, :], g2_d[:, :], "s_g1")
        k.dma("sp", cw[:, :, :], cw_d[:, :, :], "s_gup")
        k.dma("sp", cb[:, :], cb_d[:, :], "s_negb")
        k.dma("sp", wpost[:, :], wpost2_d[:, :], "wpost")
        xa = [sb2("xa%d" % i, [128, D]) for i in range(2)]
        h2tok = [sb2("h2tok%d" % i, [128, D], BF16) for i in range(1)]
        junk2 = sb2("junk2", [128, D], BF16)
        ssq2 = sb2("ssq2", [128, 8])
        lnv2 = sb2("lnv2", [128, 8])
        rstd2 = sb2("rstd2", [128, 8])
        h2T = sb2("h2T", [128, 8, TS], BF16)
        gT = sb2("gT", [128, NFC, TS], BF16)
        tv = [sb2("tv%d" % i, [128, TS]) for i in range(2)]
        tg = [sb2("tg%d" % i, [128, TS]) for i in range(2)]
        ge = [sb2("ge%d" % i, [128, TS], BF16) for i in range(2)]
        halo = [sb2("halo%d" % i, [128, 2 * NFC, 2]) for i in range(2)]
        fix = sb2("fix", [128, 2 * NFC, 2])
        fta = sb2("fta", [128, 2 * NFC])
        ftb = sb2("ftb", [128, 2 * NFC])
        ytmp2 = sb2("ytmp2", [128, D])
        obuf = [sb2("obuf%d" % i, [128, D]) for i in range(2)]
        memset("dve", halo[0][:, :, :], 0.0)
        memset("dve", halo[1][:, :, :], 0.0)

        def p2_An(s, j):
            r0 = s * TS + j * 128
            xin = xa[j % 2]
            k.dma("sp", xin[:, :], x1_d[r0:r0 + 128, :], "xa%d" % (j % 2))
            ht = h2tok[0]
            act(junk2[:, :], xin[:, :], AF.Square, accum=ssq2[:, j:j + 1])
            rstd_from_ss(ssq2[:, j:j + 1], lnv2[:, j:j + 1], rstd2[:, j:j + 1], D)
            act(ht[:, :], xin[:, :], AF.Identity, scale=rstd2[:, j:j + 1])

        def p2_At(s, j):
            ht = h2tok[0]
            for kk in range(8):
                tr(P0bf[:, kk * 128:(kk + 1) * 128], ht[:, kk * 128:(kk + 1) * 128], ident_bf[:, :])
            cp("dve", h2T[:, :, j * 128:(j + 1) * 128],
               P0bf[:, 0:1024].rearrange("p (a t) -> p a t", a=8))

        cbank = [0]

        def p2_C(s, j):
            tsl = slice(j * 128, (j + 1) * 128)
            r0 = s * TS + j * 128
            ob = obuf[j % 2]
            k.dma("sp", ob[:, :], x1_d[r0:r0 + 128, :], "ost%d" % (j % 2))
            banks = []
            for hf in range(2):
                bank = 5 + cbank[0] % 3
                cbank[0] += 1
                banks.append(bank)
                for i in range(NFC):
                    mm(PS[bank][:, :], gT[:, i, tsl], wdown[:, i, hf * 512:(hf + 1) * 512],
                       start=(i == 0), stop=(i == NFC - 1))
                act(junk2[:, hf * 512:(hf + 1) * 512], PS[bank][:, :], AF.Square, accum=ssq2[:, 4 + hf:5 + hf])
            tt("dve", ssq2[:, 6:7], ssq2[:, 4:5], ssq2[:, 5:6], ALU.add)
            rstd_from_ss(ssq2[:, 6:7], lnv2[:, 6:7], rstd2[:, 6:7], D)
            for hf in range(2):
                stt(ytmp2[:, hf * 512:(hf + 1) * 512], PS[banks[hf]][:, :], rstd2[:, 6:7],
                    wpost[:, hf * 512:(hf + 1) * 512], ALU.mult, ALU.mult)
            tt("pool", ob[:, :], ytmp2[:, :], ob[:, :], ALU.add)
            k.dma("pool", out_d[r0:r0 + 128, :], ob[:, :], "ost%d" % (j % 2), is_output=True)

        ring = [ytmp2, obuf[0], obuf[1]]
        rcnt = [0]
        g2b = g2[:, :].unsqueeze(2).broadcast_to([128, 8, 128])

        def load_wup_chunk(c):
            r = rcnt[0] % 3
            rcnt[0] += 1
            st = ring[r]
            k.dma("sp", st[:, :], wup_d[c, :, :], "ring%d" % r)
            tt("dve" if c < NFC else "pool", wup[:, :, c * 128:(c + 1) * 128],
               st[:, :].rearrange("p (a n) -> p a n", a=8), g2b, ALU.mult)

        def load_wdown_chunk(i):
            r = rcnt[0] % 3
            rcnt[0] += 1
            st = ring[r]
            k.dma("sp", st[:, :], wdown_d[:, i, :], "ring%d" % r)
            cp("act", wdown[:, i, :], st[:, :])

        def p2_B(s):
            hprev = halo[(s + 1) % 2]
            hnext = halo[s % 2]
            tt("dve", fta[:, :], cw[:, :, 1], hprev[:, :, 1], ALU.mult)
            tt("dve", ftb[:, :], cw[:, :, 0], hprev[:, :, 0], ALU.mult)
            tt("dve", fix[:, :, 0], fta[:, :], ftb[:, :], ALU.add)
            tt("dve", fix[:, :, 1], cw[:, :, 0], hprev[:, :, 1], ALU.mult)
            def gelu_mul(i):
                act(ge[i % 2][:, :], tg[i % 2][:, :], AF.Gelu_apprx_tanh)
                tt("pool", gT[:, i, :], ge[i % 2][:, :], tv[i % 2][:, :], ALU.mult)

            for i in range(NFC):
                if s == 0:
                    if i == 0:
                        load_wup_chunk(0)
                        load_wup_chunk(NFC)
                    if i + 1 < NFC:
                        load_wup_chunk(i + 1)
                        load_wup_chunk(NFC + i + 1)
                pb = 1 + 2 * (i % 2)
                pend = []
                for (bank, cidx, tbuf) in ((pb, i, tv[i % 2]), (pb + 1, NFC + i, tg[i % 2])):
                    for kk in range(8):
                        mm(PS[bank][:, :], wup[:, kk, cidx * 128:(cidx + 1) * 128], h2T[:, kk, :],
                           start=(kk == 0), stop=(kk == 7))
                    act(tbuf[:, :], PS[bank][:, :], AF.Identity, bias=cb[:, cidx:cidx + 1],
                        scale=cw[:, cidx, 2:3])
                    cp("act", hnext[:, cidx, :], PS[bank][:, TS - 2:TS])
                    pend.append((tbuf, bank, cidx))
                if i > 0:
                    gelu_mul(i - 1)
                for (tbuf, bank, cidx) in pend:
                    stt(tbuf[:, 1:TS], PS[bank][:, 0:TS - 1], cw[:, cidx, 1:2], tbuf[:, 1:TS], ALU.mult, ALU.add)
                    stt(tbuf[:, 2:TS], PS[bank][:, 0:TS - 2], cw[:, cidx, 0:1], tbuf[:, 2:TS], ALU.mult, ALU.add)
                    tt("dve", tbuf[:, 0:2], tbuf[:, 0:2], fix[:, cidx, :], ALU.add)
            gelu_mul(NFC - 1)
            if s == 0:
                for i in range(NFC):
                    load_wdown_chunk(i)

        for j in range(NSUB):
            p2_An(0, j)
            p2_At(0, j)
        for s in range(NS):
            p2_B(s)
            for j in range(NSUB):
                if s + 1 < NS:
                    p2_An(s + 1, j)
                p2_C(s, j)
                if s + 1 < NS:
                    p2_At(s + 1, j)

    k.finish()
    es.close()
    return nc, k


def _consts():
    c = np.zeros((128, 896), np.float32)
    c[:, 0:128] = np.eye(128, dtype=np.float32)
    p = np.arange(128)[:, None]
    i = np.arange(128)[None, :]
    m = (p <= i).astype(np.float32)
    c[:, 128:256] = m
    c[:, 256:384] = m
    sm = np.ones((128, 512), np.float32)
    sm[:, 0::128] = 0.0
    c[:, 384:896] = sm
    return c


def _chunk_rows(w):
    kc = w.shape[0] // 128
    return np.ascontiguousarray(w.reshape(kc, 128, w.shape[1]).transpose(1, 0, 2))


def _col(v):
    return np.ascontiguousarray(v.reshape(-1, 128).T)


def prep_shared(inp):
    f = lambda a: np.asarray(a, dtype=np.float32)
    w_in = f(inp["w_in"])[0]
    gq, gk, gv = w_in[:, 0:256], w_in[:, 256:512], w_in[:, 512:1024]
    glr, gg, sq = w_in[:, 1024:1040], w_in[:, 1040:1552], w_in[:, 1552:2064]
    sk, sv = w_in[:, 2064:2192], w_in[:, 2192:2320]
    skd = np.concatenate([sk[:, 0:64], sk[:, 0:64], sk[:, 64:128], sk[:, 64:128]], axis=1)
    w1 = np.concatenate([gq, gk, gg, glr, gv, sq, skd, sv], axis=1)
    assert w1.shape[1] == NC1
    sinks = f(inp["swa_sinks"])[0]
    sink_l = np.zeros((128, 4), np.float32)
    for cc in range(4):
        sink_l[0:64, cc] = sinks[2 * cc]
        sink_l[64:128, cc] = sinks[2 * cc + 1]
    conv_w = f(inp["conv_w"])[0]
    cw = np.ascontiguousarray(conv_w.T.reshape(2 * NFC, 128, 3).transpose(1, 0, 2))
    return {
        "cst": _consts(),
        "w1": _chunk_rows(w1),
        "g1": _col(f(inp["pre_mix_norm"])[0]),
        "gup": np.ascontiguousarray(f(inp["gla_gate_up"])[0]),
        "gbias": _col(f(inp["gla_gate_bias"])[0]),
        "gnorm": np.ascontiguousarray(f(inp["gla_out_norm"])[0].reshape(128, 1)),
        "sink": sink_l,
        "wout": _chunk_rows(f(inp["w_out"])[0]),
        "wpost1": np.ascontiguousarray(np.broadcast_to(f(inp["post_mix_norm"])[0][None, :], (128, D))),
        "g2": _col(f(inp["pre_ffn_norm"])[0]),
        "wup": np.ascontiguousarray(
            f(inp["w_up"])[0].reshape(8, 128, 2 * NFC, 128).transpose(2, 1, 0, 3).reshape(2 * NFC, 128, D)),
        "cw": cw,
        "cb": _col(f(inp["conv_b"])[0]),
        "wdown": _chunk_rows(f(inp["w_down"])[0]),
        "wpost2": np.ascontiguousarray(np.broadcast_to(f(inp["post_ffn_norm"])[0][None, :], (128, D))),
    }


def core_inputs(shared, x_b, pos_b):
    m = dict(shared)
    m["x"] = np.ascontiguousarray(np.asarray(x_b, dtype=np.float32))
    m["pos"] = np.ascontiguousarray(np.asarray(pos_b, dtype=np.int32).reshape(-1, 128).T)
    return m


_NC_CACHE = {}


def kernel(**inputs):
    x = np.asarray(inputs["x"])
    B, T, _ = x.shape
    pos = np.asarray(inputs["positions"])
    shared = prep_shared(inputs)
    if T not in _NC_CACHE:
        _NC_CACHE[T] = build_nc(T)[0]
    nc = _NC_CACHE[T]
    in_maps = [core_inputs(shared, x[b], pos[b]) for b in range(B)]
    res = run_bass_kernel_spmd(nc, in_maps, core_ids=list(range(B)))
    return np.stack([np.asarray(r["out"]) for r in res.results], axis=0).astype(np.float32)
```

```python
import math
from contextlib import ExitStack

import numpy as np
import concourse.bass as bass
import concourse.mybir as mybir
from concourse.bass_utils import run_bass_kernel_spmd

F32 = mybir.dt.float32
BF16 = mybir.dt.bfloat16
I32 = mybir.dt.int32
AF = mybir.ActivationFunctionType
ALU = mybir.AluOpType

D = 1024
T_FULL = 8192
TS = 512
NSUB = TS // 128
DFF = 2816
NFC = DFF // 128
EPS = 1e-6
ROPE_THETA = 500000.0

C_GQ, C_GK, C_GG, C_GLR = 0, 256, 512, 1024
C_TM = 1040
C_GV, C_SQ, C_SK, C_SV = C_TM, C_TM + 512, C_TM + 1024, C_TM + 1280
NC1 = C_TM + 1408

SAME_ENGINE_SYNC = True


class K:
    def __init__(self, nc, es):
        self.nc = nc
        self.es = es
        self.eng = dict(pe=nc.tensor, act=nc.scalar, dve=nc.vector, pool=nc.gpsimd, sp=nc.sync)
        self.sems = {}
        self.cnt = {}
        for e in ("pe", "act", "dve", "pool"):
            self.sems[e] = es.enter_context(nc.semaphore("sem_" + e))
            self.cnt[e] = 0
        self.seen = {e: {} for e in self.eng}
        self.recs = {}
        self.untracked = set()
        self.n_inst = 0
        self.out_tokens = {}

    def dma_sem(self, key):
        if key not in self.sems:
            self.sems[key] = self.es.enter_context(self.nc.semaphore("sd_" + key))
            self.cnt[key] = 0
        return key

    @staticmethod
    def box(ap):
        t = ap.tensor
        name = t.name
        dims = tuple(ap.ap)
        space = str(ap.space)
        if "PSUM" in space.upper():
            return name, 0, 128, 0, 1 << 30
        if "SB" in space.upper():
            ps = 1
            for v in list(t.shape)[1:]:
                ps *= int(v)
            off = int(ap.offset)
            p0 = off // ps
            lo = off % ps
            npart = int(dims[0][1])
            hi = lo + sum((int(c) - 1) * abs(int(s)) for s, c in dims[1:]) + 1
            return name, p0, p0 + npart, lo, hi
        off = int(ap.offset)
        hi = off + sum((int(c) - 1) * abs(int(s)) for s, c in dims) + 1
        return name, 0, 1, off, hi

    def _deps(self, eng, reads, writes):
        needs = {}

        def need(tok):
            k, v = tok
            if needs.get(k, 0) < v:
                needs[k] = v

        rb = [self.box(a) for a in reads]
        wb = [self.box(a) for a in writes]
        for (name, p0, p1, lo, hi) in rb:
            if name in self.untracked:
                continue
            for r in self.recs.get(name, ()):
                if r[5] and r[0] < p1 and p0 < r[1] and r[2] < hi and lo < r[3]:
                    need(r[4])
        for (name, p0, p1, lo, hi) in wb:
            if name in self.untracked:
                continue
            for r in self.recs.get(name, ()):
                if r[0] < p1 and p0 < r[1] and r[2] < hi and lo < r[3]:
                    need(r[4])
        e = self.eng[eng]
        for k, v in needs.items():
            if k == eng and (eng == "pe" or not SAME_ENGINE_SYNC):
                continue
            if self.seen[eng].get(k, 0) >= v:
                continue
            e.wait_ge(self.sems[k], v)
            self.n_inst += 1
            self.seen[eng][k] = v
        return rb, wb

    def _record(self, tok, rb, wb):
        for (name, p0, p1, lo, hi) in wb:
            if name in self.untracked:
                continue
            lst = self.recs.setdefault(name, [])
            lst[:] = [r for r in lst if not (p0 <= r[0] and r[1] <= p1 and lo <= r[2] and r[3] <= hi)]
            lst.append((p0, p1, lo, hi, tok, True))
        for (name, p0, p1, lo, hi) in rb:
            if name in self.untracked:
                continue
            lst = self.recs.setdefault(name, [])
            lst[:] = [r for r in lst if not ((not r[5]) and r[4][0] == tok[0] and p0 <= r[0] and r[1] <= p1
                                             and lo <= r[2] and r[3] <= hi)]
            lst.append((p0, p1, lo, hi, tok, False))

    def op(self, eng, fn, reads, writes):
        rb, wb = self._deps(eng, reads, writes)
        inst = fn()
        self.cnt[eng] += 1
        inst.then_inc(self.sems[eng], 1)
        self.n_inst += 1
        self._record((eng, self.cnt[eng]), rb, wb)

    def dma(self, q, out, in_, semkey, is_output=False):
        self.dma_sem(semkey)
        rb, wb = self._deps(q, [in_], [out])
        inst = self.eng[q].dma_start(out=out, in_=in_)
        self.cnt[semkey] += 16
        inst.then_inc(self.sems[semkey], 16)
        self.n_inst += 1
        tok = (semkey, self.cnt[semkey])
        self._record(tok, rb, wb)
        if is_output:
            self.out_tokens[semkey] = (q, self.cnt[semkey])

    def barrier(self):
        for e in ("pe", "act", "dve", "pool", "sp"):
            for key, sem in self.sems.items():
                v = self.cnt[key]
                if v > 0 and self.seen[e].get(key, 0) < v:
                    if key == e and e == "pe":
                        continue
                    self.eng[e].wait_ge(sem, v)
                    self.seen[e][key] = v
                    self.n_inst += 1
        self.recs.clear()

    def finish(self):
        for k, (q, v) in self.out_tokens.items():
            self.eng["sp"].wait_ge(self.sems[k], v)


def build_nc(T=T_FULL, debug_x1=False):
    NS = T // TS
    NBLK = T // 128
    nc = bass.Bass("TRN2", target_bir_lowering=False)

    def din(name, shape, dt=F32):
        return nc.dram_tensor(name, list(shape), dt, kind="ExternalInput").ap()

    x_d = din("x", [T, D])
    pos_d = din("pos", [128, NBLK], I32)
    cst_d = din("cst", [128, 896])
    w1_d = din("w1", [128, 8, NC1])
    g1_d = din("g1", [128, 8])
    gup_d = din("gup", [16, 256])
    gbias_d = din("gbias", [128, 2])
    gnorm_d = din("gnorm", [128, 1])
    sink_d = din("sink", [128, 4])
    wout_d = din("wout", [128, 8, D])
    wpost1_d = din("wpost1", [128, D])
    g2_d = din("g2", [128, 8])
    wup_d = din("wup", [2 * NFC, 128, D])
    cw_d = din("cw", [128, 2 * NFC, 3])
    cb_d = din("cb", [128, 2 * NFC])
    wdown_d = din("wdown", [128, NFC, D])
    wpost2_d = din("wpost2", [128, D])
    out_d = nc.dram_tensor("out", [T, D], F32, kind="ExternalOutput").ap()
    x1_d = nc.dram_tensor("x1s", [T, D], F32,
                          kind="ExternalOutput" if debug_x1 else "Internal").ap()

    es = ExitStack()
    k = K(nc, es)
    k.untracked.update(["x", "pos", "cst", "w1", "g1", "gup", "gbias", "gnorm", "sink", "wout",
                        "wpost1", "g2", "wup", "cw", "cb", "wdown", "wpost2", "out"])

    def sb(name, shape, dt=F32):
        return es.enter_context(nc.sbuf_tensor(name, list(shape), dt))

    def mm(out, lhsT, rhs, start=True, stop=True):
        k.op("pe", lambda: nc.tensor.matmul(out, lhsT, rhs, start=start, stop=stop), [lhsT, rhs], [out])

    def tr(out, in_, ident):
        k.op("pe", lambda: nc.tensor.transpose(out, in_, ident), [in_, ident], [out])

    def act(out, in_, func, bias=None, scale=None, accum=None):
        kw = {}
        rd = [in_]
        wr = [out]
        if bias is not None:
            kw["bias"] = bias
            if not isinstance(bias, (int, float)):
                rd.append(bias)
        if scale is not None:
            kw["scale"] = scale
            if not isinstance(scale, (int, float)):
                rd.append(scale)
        if accum is not None:
            kw["accum_out"] = accum
            wr.append(accum)
        k.op("act", lambda: nc.scalar.activation(out=out, in_=in_, func=func, **kw), rd, wr)

    def tt(eng, out, in0, in1, op):
        e = k.eng[eng]
        k.op(eng, lambda: e.tensor_tensor(out=out, in0=in0, in1=in1, op=op), [in0, in1], [out])

    def ts(eng, out, in0, s1, op0, s2=None, op1=None):
        e = k.eng[eng]
        rd = [in0] + [s for s in (s1, s2) if s is not None and not isinstance(s, (int, float))]
        if op1 is None:
            k.op(eng, lambda: e.tensor_scalar(out=out, in0=in0, scalar1=s1, scalar2=None, op0=op0), rd, [out])
        else:
            k.op(eng, lambda: e.tensor_scalar(out=out, in0=in0, scalar1=s1, scalar2=s2, op0=op0, op1=op1),
                 rd, [out])

    def stt(out, in0, scalar, in1, op0, op1):
        rd = [in0, in1] + ([] if isinstance(scalar, (int, float)) else [scalar])
        k.op("dve", lambda: nc.vector.scalar_tensor_tensor(out=out, in0=in0, scalar=scalar, in1=in1,
                                                           op0=op0, op1=op1), rd, [out])

    def cp(eng, out, in_):
        if eng == "act":
            act(out, in_, AF.Copy)
        else:
            e = k.eng[eng]
            k.op(eng, lambda: e.tensor_copy(out=out, in_=in_), [in_], [out])

    def memset(eng, ap, val):
        e = k.eng[eng]
        k.op(eng, lambda: e.memset(ap, val), [], [ap])

    def rstd_from_ss(ss_ap, lnv_ap, rstd_ap, n):
        act(lnv_ap, ss_ap, AF.Ln, bias=eps_c[:, 0:1], scale=1.0 / n)
        act(rstd_ap, lnv_ap, AF.Exp, scale=-0.5)

    PS = [es.enter_context(nc.psum_tensor("ps%d" % i, [128, 512], F32)) for i in range(8)]
    P0bf = PS[0][:, :].bitcast(BF16)

    ident_bf = sb("ident_bf", [128, 128], BF16)
    ones_bf = sb("ones_bf", [128, 128], BF16)
    eps_c = sb("eps_c", [128, 1])
    one_c = sb("one_c", [128, 1])
    memset("dve", ones_bf[:, :], 1.0)
    memset("dve", eps_c[:, :], EPS)
    memset("dve", one_c[:, :], 1.0)

    wpost = sb("wpost", [128, D])
    k.dma("sp", wpost[:, :], wpost1_d[:, :], "wpost")

    with ExitStack() as es1:
        def sb1(name, shape, dt=F32):
            return es1.enter_context(nc.sbuf_tensor(name, list(shape), dt))

        cst = sb1("cst_sb", [128, 896])
        maskA = sb1("maskA", [128, 256], BF16)
        scanmask = cst[:, 384:896]
        k.dma("sp", cst[:, :], cst_d[:, :], "cst")
        cp("dve", ident_bf[:, :], cst[:, 0:128])
        cp("dve", maskA[:, :], cst[:, 128:384])

        w1 = sb1("w1_bf", [128, 8, NC1], BF16)
        wout = sb1("wout_bf", [128, 8, D], BF16)
        g1 = sb1("g1_sb", [128, 8])
        gup = sb1("gup_sb", [16, 256])
        negb = sb1("negb", [128, 2])
        gnorm = sb1("gnorm_sb", [128, 1])
        sinkexp = sb1("sinkexp", [128, 4])
        sinkfull = sb1("sinkfull", [128, 512])
        SC = sb1("sincos", [128, 16, NBLK])

        es1a = ExitStack()

        def sba(name, shape, dt=F32):
            return es1a.enter_context(nc.sbuf_tensor(name, list(shape), dt))

        stage = [sba("stage%d" % i, [128, NC1]) for i in range(2)]
        zeros_c = sba("zeros_c", [128, 128])
        posi = sba("posi", [128, NBLK], I32)
        posf = sba("posf", [128, NBLK])
        rt_y = sba("rt_y", [128, 16, NBLK])
        rt_n = sba("rt_n", [128, 16, NBLK], I32)
        rt_f = sba("rt_f", [128, 16, NBLK])
        rt_g = sba("rt_g", [128, 16, NBLK])
        memset("dve", zeros_c[:, :], 0.0)

        k.dma("sp", g1[:, :], g1_d[:, :], "s_g1")
        k.dma("sp", gup[:, :], gup_d[:, :], "s_gup")
        k.dma("sp", negb[:, :], gbias_d[:, :], "s_negb")
        k.dma("sp", gnorm[:, :], gnorm_d[:, :], "s_gnorm")
        k.dma("sp", sinkexp[:, :], sink_d[:, :], "s_sink")
        k.dma("sp", posi[:, :], pos_d[:, :], "s_pos")
        ts("dve", negb[:, :], negb[:, :], -1.0, ALU.mult)
        act(sinkexp[:, :], sinkexp[:, :], AF.Exp)
        for cc in range(4):
            ts("dve", sinkfull[:, cc * 128:(cc + 1) * 128], zeros_c[:, :], sinkexp[:, cc:cc + 1], ALU.add)

        cp("dve", posf[:, :], posi[:, :])
        for f in range(8):
            inv = ROPE_THETA ** (-(f * 2.0 / 16.0)) / (2.0 * math.pi)
            ts("dve", rt_y[:, f, :], posf[:, :], float(inv), ALU.mult)
            ts("dve", rt_y[:, 8 + f, :], posf[:, :], float(inv), ALU.mult, 0.25, ALU.add)
        cp("dve", rt_n[:, :, :], rt_y[:, :, :])
        cp("dve", rt_f[:, :, :], rt_n[:, :, :])
        tt("dve", rt_f[:, :, :], rt_y[:, :, :], rt_f[:, :, :], ALU.subtract)
        ts("dve", rt_g[:, :, :], rt_f[:, :, :], 0.5, ALU.is_gt)
        tt("dve", rt_f[:, :, :], rt_f[:, :, :], rt_g[:, :, :], ALU.subtract)
        ts("dve", rt_g[:, :, :], rt_f[:, :, :], -0.5, ALU.is_lt)
        tt("dve", rt_f[:, :, :], rt_f[:, :, :], rt_g[:, :, :], ALU.add)
        act(SC[:, :, :], rt_f[:, :, :], AF.Sin, scale=2.0 * math.pi * (1.0 - 1e-6))

        for kk in range(8):
            st = stage[kk % 2]
            k.dma("sp", st[:, :], w1_d[:, kk, :], "stage%d" % (kk % 2))
            if kk % 2 == 0:
                ts("dve", w1[:, kk, :], st[:, :], g1[:, kk:kk + 1], ALU.mult)
            else:
                act(w1[:, kk, :], st[:, :], AF.Identity, scale=g1[:, kk:kk + 1])
        for kk in range(0, 8, 2):
            st = stage[(kk // 2) % 2]
            k.dma("sp", st[:, 0:2048].rearrange("p (a d) -> p a d", a=2), wout_d[:, kk:kk + 2, :],
                  "stage%d" % ((kk // 2) % 2))
            if (kk // 2) % 2 == 0:
                cp("dve", wout[:, kk:kk + 2, :], st[:, 0:2048].rearrange("p (a d) -> p a d", a=2))
            else:
                cp("act", wout[:, kk:kk + 2, :], st[:, 0:2048].rearrange("p (a d) -> p a d", a=2))
        k.barrier()
        es1a.close()

        xa1 = [sb1("xa1_%d" % i, [128, D]) for i in range(2)]
        htok = [sb1("htok%d" % i, [128, D], BF16) for i in range(3)]
        junk = sb1("junk", [128, D], BF16)
        ssq = sb1("ssq", [128, 8])
        lnv = sb1("lnv", [128, 8])
        rstd = sb1("rstd", [128, 8])
        hT = sb1("hT", [128, 8, TS], BF16)
        glrT = sb1("glrT", [16, TS])
        sp_t = sb1("sp_t", [128, 2, TS])
        bc_t = sb1("bc_t", [128, 2, TS])
        Eb = sb1("Eb", [128, 2, TS])
        Enb = sb1("Enb", [128, 2, TS])
        eblast = [sb1("eblast%d" % i, [128, 2, NSUB]) for i in range(2)]
        qg = [sb1("qg%d" % i, [128, 2, TS], BF16) for i in range(2)]
        kg = [sb1("kg%d" % i, [128, 2, TS], BF16) for i in range(2)]
        sgT = [sb1("sgT%d" % i, [128, 4, TS], BF16) for i in range(2)]
        v_g = [sb1("v_g%d" % i, [128, NSUB, 512], BF16) for i in range(2)]
        qT_s = [sb1("qT_s%d" % i, [128, NSUB, 4, 128], BF16) for i in range(2)]
        qk32 = [sb1("qk32_%d" % i, [128, 12, 16]) for i in range(2)]
        gx = [sb1("gx%d" % i, [128, TS]) for i in range(4)]
        rtmp = [sb1("rtmp%d" % i, [128, 12, 8]) for i in range(4)]
        qtok = [sb1("qtok%d" % i, [128, 12, 64], BF16) for i in range(2)]
        kT_s = sb1("kT_s", [128, 8, 2, 128], BF16)
        v_s = sb1("v_s", [128, 8, 128], BF16)
        pT = [sb1("pT%d" % i, [128, 512], BF16) for i in range(4)]
        den_sb = sb1("den_sb", [128, 512])
        AT_sb = [sb1("AT_sb%d" % i, [128, 256], BF16) for i in range(2)]
        ktok = sb1("ktok", [128, 256], BF16)
        S_st = sb1("S_st", [128, 2, 128])
        S_tmp = sb1("S_tmp", [128, 2, 128])
        S_pad = sb1("S_pad", [128, 4, 128], BF16)
        sq_bf = sb1("sq_bf", [128, 512], BF16)
        gl_rs = sb1("gl_rs", [128, 512])
        gl_on = sb1("gl_on", [128, 512])
        mixT = sb1("mixT", [128, 8, TS], BF16)
        ytmp = sb1("ytmp", [128, D])
        x1buf = [sb1("x1buf%d" % i, [128, D]) for i in range(2)]

        memset("dve", S_st[:, :, :], 0.0)
        memset("dve", S_pad[:, :, :], 0.0)
        P6bf = PS[6][:, :].bitcast(BF16)
        fbank = [0]

        def nb():
            fbank[0] += 1
            return 1 + fbank[0] % 2

        def front(s):
            par = s % 2

            def An(j):
                r0 = s * TS + j * 128
                xin = xa1[j % 2]
                ht = htok[j % 3]
                k.dma("sp", xin[:, :], x_d[r0:r0 + 128, :], "xa1_%d" % (j % 2))
                act(junk[:, :], xin[:, :], AF.Square, accum=ssq[:, j:j + 1])
                rstd_from_ss(ssq[:, j:j + 1], lnv[:, j:j + 1], rstd[:, j:j + 1], D)
                ts("dve", ht[:, :], xin[:, :], rstd[:, j:j + 1], ALU.mult)

            def At_pe(j):
                ht = htok[j % 3]
                for kk in range(8):
                    tr(P0bf[:, kk * 128:(kk + 1) * 128], ht[:, kk * 128:(kk + 1) * 128], ident_bf[:, :])

            def At_ew(j):
                cp("dve", hT[:, :, j * 128:(j + 1) * 128],
                   P0bf[:, 0:1024].rearrange("p (a t) -> p a t", a=8))

            def fm_chunk(bank, col0, m):
                for kk in range(8):
                    mm(PS[bank][0:m, :], w1[:, kk, col0:col0 + m], hT[:, kk, :], start=(kk == 0), stop=(kk == 7))

            for j in range(NSUB + 2):
                if j < NSUB:
                    An(j)
                if j >= 2:
                    At_pe(j - 2)
                yield
                if j >= 2:
                    At_ew(j - 2)
                    yield
            b = nb()
            fm_chunk(b, C_GLR, 16)
            yield
            cp("act", glrT[:, :], PS[b][0:16, :])
            yield
            bl = [nb(), nb()]
            for c in range(2):
                mm(PS[bl[c]][:, :], gup[:, c * 128:(c + 1) * 128], glrT[:, :])
            yield
            for c in range(2):
                act(sp_t[:, c, :], PS[bl[c]][:, :], AF.Exp, bias=negb[:, c:c + 1], scale=-1.0)
            for c in range(2):
                act(sp_t[:, c, :], sp_t[:, c, :], AF.Ln, bias=one_c[:, 0:1], scale=1.0)
            for c in range(2):
                k.op("dve", lambda c=c: nc.vector.tensor_tensor_scan(
                    out=bc_t[:, c, :], data0=scanmask, data1=sp_t[:, c, :], initial=0.0,
                    op0=ALU.mult, op1=ALU.add), [scanmask, sp_t[:, c, :]], [bc_t[:, c, :]])
            yield
            bg = [nb(), nb()]
            for h in range(2):
                fm_chunk(bg[h], C_GG + h * 128, 128)
            yield
            for c in range(2):
                act(Eb[:, c, :], bc_t[:, c, :], AF.Exp, scale=-1.0 / 16.0)
                act(Enb[:, c, :], bc_t[:, c, :], AF.Exp, scale=1.0 / 16.0)
            cp("pool", eblast[par][:, :, :],
               Eb[:, :, :].rearrange("p c (j t) -> p c j t", t=128)[:, :, :, 127])
            for h in range(2):
                cp("dve", gx[h][:, :], PS[bg[h]][:, :])
            yield
            bg = [nb(), nb()]
            for h in range(2):
                fm_chunk(bg[h], C_GG + (2 + h) * 128, 128)
            yield
            for h in range(2):
                cp("dve", gx[2 + h][:, :], PS[bg[h]][:, :])
            yield
            bq = [nb(), nb()]
            for c in range(2):
                fm_chunk(bq[c], C_GQ + c * 128, 128)
            yield
            for c in range(2):
                stt(qg[par][:, c, :], PS[bq[c]][:, :], 0.125, Eb[:, c, :], ALU.mult, ALU.mult)
            for h in range(4):
                act(sgT[par][:, h, :], gx[h][:, :], AF.Silu)
            yield
            bk = [nb(), nb()]
            for c in range(2):
                fm_chunk(bk[c], C_GK + c * 128, 128)
            yield
            for c in range(2):
                tt("dve", kg[par][:, c, :], PS[bk[c]][:, :], Enb[:, c, :], ALU.mult)
            yield

            def Ct_pe(j):
                qflat = qtok[j % 2][:, :, :].rearrange("p h d -> p (h d)")
                for cc in range(6):
                    tr(P0bf[:, cc * 128:(cc + 1) * 128], qflat[:, cc * 128:(cc + 1) * 128], ident_bf[:, :])

            def Ct_ew(j):
                slot = (s * NSUB + j) % 8
                cp("dve", qT_s[par][:, j, :, :], P0bf[:, 0:512].rearrange("p (a t) -> p a t", a=4))
                cp("dve", kT_s[:, slot, :, :], P0bf[:, 512:768].rearrange("p (a t) -> p a t", a=2))

            for j in range(NSUB):
                gb = s * NSUB + j
                slot = gb % 8
                q32 = qk32[j % 2]
                qtk = qtok[j % 2]
                lhs = lambda kk: hT[:, kk, j * 128:(j + 1) * 128]
                b1, b2 = nb(), nb()
                for kk in range(8):
                    mm(PS[b1][:, :], lhs(kk), w1[:, kk, C_GV:C_GV + 512], kk == 0, kk == 7)
                for kk in range(8):
                    mm(PS[b2][:, :], lhs(kk), w1[:, kk, C_SQ:C_SQ + 512], kk == 0, kk == 7)
                yield
                cp("act", v_g[par][:, j, :], PS[b1][:, :])
                pq = PS[b2][:, :].rearrange("p (h d) -> p h d", d=64)
                cp("act", q32[:, 0:8, :], pq[:, :, 0:16])
                cp("act", qtk[:, 0:8, 16:64], pq[:, :, 16:64])
                if j > 0:
                    Ct_pe(j - 1)
                b3 = nb()
                for kk in range(8):
                    mm(PS[b3][:, 0:384], lhs(kk), w1[:, kk, C_SK:C_SK + 384], kk == 0, kk == 7)
                yield
                pk = PS[b3][:, 0:256].rearrange("p (h d) -> p h d", d=64)
                cp("act", q32[:, 8:12, :], pk[:, :, 0:16])
                cp("act", qtk[:, 8:12, 16:64], pk[:, :, 16:64])
                cp("act", v_s[:, slot, :], PS[b3][:, 256:384])
                if j > 0:
                    Ct_ew(j - 1)
                sinb = SC[:, 0:8, gb].unsqueeze(1).broadcast_to([128, 12, 8])
                cosb = SC[:, 8:16, gb].unsqueeze(1).broadcast_to([128, 12, 8])
                x1v = q32[:, :, 0:8]
                x2v = q32[:, :, 8:16]
                tt("dve", rtmp[0][:, :, :], x1v, cosb, ALU.mult)
                tt("dve", rtmp[1][:, :, :], x2v, sinb, ALU.mult)
                tt("dve", qtk[:, :, 0:8], rtmp[0][:, :, :], rtmp[1][:, :, :], ALU.subtract)
                tt("dve", rtmp[2][:, :, :], x2v, cosb, ALU.mult)
                tt("dve", rtmp[3][:, :, :], x1v, sinb, ALU.mult)
                tt("dve", qtk[:, :, 8:16], rtmp[2][:, :, :], rtmp[3][:, :, :], ALU.add)
                yield
            Ct_pe(NSUB - 1)
            yield
            Ct_ew(NSUB - 1)
            yield

        def back(s):
            par = s % 2

            def F_pe(j):
                tsl = slice(j * 128, (j + 1) * 128)
                fb = (5, 7)
                for hf in range(2):
                    for kk in range(8):
                        mm(PS[fb[hf]][:, :], mixT[:, kk, tsl], wout[:, kk, hf * 512:(hf + 1) * 512],
                           start=(kk == 0), stop=(kk == 7))

            def F_ew(j):
                r0 = s * TS + j * 128
                xb = x1buf[j % 2]
                fb = (5, 7)
                for hf in range(2):
                    act(junk[:, hf * 512:(hf + 1) * 512], PS[fb[hf]][:, :], AF.Square,
                        accum=ssq[:, 4 + hf:5 + hf])
                tt("dve", ssq[:, 6:7], ssq[:, 4:5], ssq[:, 5:6], ALU.add)
                rstd_from_ss(ssq[:, 6:7], lnv[:, 6:7], rstd[:, 6:7], D)
                for hf in range(2):
                    stt(ytmp[:, hf * 512:(hf + 1) * 512], PS[fb[hf]][:, :], rstd[:, 6:7],
                        wpost[:, hf * 512:(hf + 1) * 512], ALU.mult, ALU.mult)
                tt("dve", xb[:, :], ytmp[:, :], xb[:, :], ALU.add)
                k.dma("pool", x1_d[r0:r0 + 128, :], xb[:, :], "x1st%d" % (j % 2), is_output=debug_x1)

            for j in range(NSUB):
                gb = s * NSUB + j
                slot = gb % 8
                pslot = (gb - 1) % 8
                tsl = slice(j * 128, (j + 1) * 128)
                kbs = ([] if gb == 0 else [(0, pslot)]) + [(1, slot)]
                for m in range(2):
                    for (kb, sl) in kbs:
                        for g in range(2):
                            mm(PS[3 + kb][:, g * 256:(g + 1) * 256],
                               kT_s[m * 64:(m + 1) * 64, sl, g, :],
                               qT_s[par][m * 64:(m + 1) * 64, j, 2 * g:2 * g + 2, :].rearrange("p a t -> p (a t)"))
                    if m == 0:
                        for c in range(2):
                            tr(P6bf[:, c * 128:(c + 1) * 128], kg[par][:, c, tsl], ident_bf[:, :])
                    yield
                    for (kb, sl) in kbs:
                        pt = pT[m * 2 + kb]
                        act(pt[:, :], PS[3 + kb][:, :], AF.Exp, scale=0.125)
                        if kb == 1:
                            k.op("pool", lambda pt=pt: nc.gpsimd.affine_select(
                                out=pt[:, :], in_=pt[:, :], pattern=[[0, 4], [1, 128]],
                                compare_op=ALU.is_ge, fill=0.0, base=0, channel_multiplier=-1),
                                [pt[:, :]], [pt[:, :]])
                        else:
                            k.op("pool", lambda pt=pt: nc.gpsimd.affine_select(
                                out=pt[:, :], in_=pt[:, :], pattern=[[0, 4], [-1, 128]],
                                compare_op=ALU.is_gt, fill=0.0, base=0, channel_multiplier=1),
                                [pt[:, :]], [pt[:, :]])
                    if m == 0:
                        cp("act", ktok[:, :], P6bf[:, 0:256])
                    yield
                if j > 0:
                    F_pe(j - 1)
                    yield
                    F_ew(j - 1)
                    yield
                for c in range(2):
                    for m in range(2):
                        mm(PS[3 + m][:, c * 128:(c + 1) * 128], kg[par][m * 64:(m + 1) * 64, c, tsl],
                           qg[par][m * 64:(m + 1) * 64, c, tsl])
                yield
                for m in range(2):
                    tt("dve", AT_sb[m][:, :], PS[3 + m][:, 0:256], maskA[:, :], ALU.mult)
                yield
                for g in range(2):
                    for m in range(2):
                        for idx, (kb, sl) in enumerate(kbs):
                            mm(PS[5][m * 64:(m + 1) * 64, g * 256:(g + 1) * 256],
                               v_s[:, sl, g * 64:(g + 1) * 64], pT[m * 2 + kb][:, g * 256:(g + 1) * 256],
                               start=(idx == 0), stop=(idx == len(kbs) - 1))
                for g in range(2):
                    for m in range(2):
                        for idx, (kb, sl) in enumerate(kbs):
                            mm(PS[6][m * 64:(m + 1) * 64, g * 256:(g + 1) * 256],
                               ones_bf[:, 0:64], pT[m * 2 + kb][:, g * 256:(g + 1) * 256],
                               start=(idx == 0), stop=(idx == len(kbs) - 1))
                yield
                tt("dve", den_sb[:, :], PS[6][:, :], sinkfull[:, :], ALU.add)
                act(den_sb[:, :], den_sb[:, :], AF.Ln)
                act(den_sb[:, :], den_sb[:, :], AF.Exp, scale=-1.0)
                tt("dve", mixT[:, 4:8, tsl],
                   PS[5][:, :].rearrange("p (a t) -> p a t", a=4),
                   den_sb[:, :].rearrange("p (a t) -> p a t", a=4), ALU.mult)
                yield
                for h in range(4):
                    c, m = h // 2, h % 2
                    mm(PS[7][:, h * 128:(h + 1) * 128], v_g[par][:, j, h * 128:(h + 1) * 128],
                       AT_sb[m][:, c * 128:(c + 1) * 128], start=True, stop=False)
                    mm(PS[7][:, h * 128:(h + 1) * 128], S_pad[:, h, :], qg[par][:, c, tsl], start=False, stop=True)
                for c in range(2):
                    for m in range(2):
                        h = 2 * c + m
                        mm(PS[3][m * 64:(m + 1) * 64, c * 128:(c + 1) * 128],
                           ktok[:, c * 128 + m * 64:c * 128 + (m + 1) * 64], v_g[par][:, j, h * 128:(h + 1) * 128])
                yield
                act(sq_bf[:, :], PS[7][:, :], AF.Square)
                tt("dve", S_tmp[:, :, :], PS[3][:, 0:256].rearrange("p (c e) -> p c e", c=2), S_st[:, :, :],
                   ALU.add)
                for c in range(2):
                    ts("dve", S_st[:, c, :], S_tmp[:, c, :], eblast[par][:, c, j:j + 1], ALU.mult)
                    for m in range(2):
                        cp("pool", S_pad[m * 64:(m + 1) * 64, 2 * c + m, :], S_st[m * 64:(m + 1) * 64, c, :])
                yield
                mm(PS[6][:, :], ones_bf[:, :], sq_bf[:, :])
                yield
                act(gl_rs[:, :], PS[6][:, :], AF.Ln, bias=eps_c[:, 0:1], scale=1.0 / 128.0)
                act(gl_rs[:, :], gl_rs[:, :], AF.Exp, scale=-0.5)
                stt(gl_on[:, :], PS[7][:, :], gnorm[:, 0:1], gl_rs[:, :], ALU.mult, ALU.mult)
                tt("pool", mixT[:, 0:4, tsl], gl_on[:, :].rearrange("p (a t) -> p a t", a=4),
                   sgT[par][:, :, tsl], ALU.mult)
                r0 = s * TS + j * 128
                k.dma("sp", x1buf[j % 2][:, :], x_d[r0:r0 + 128, :], "x1st%d" % (j % 2))
                yield
            F_pe(NSUB - 1)
            yield
            F_ew(NSUB - 1)
            yield

        def drain(g):
            for _ in g:
                pass

        def interleave(fg, bg, nf, nbk):
            fdone = 0
            for ib in range(nbk):
                try:
                    next(bg)
                except StopIteration:
                    break
                want = ((ib + 1) * nf) // nbk
                while fdone < want:
                    try:
                        next(fg)
                    except StopIteration:
                        fdone = nf
                        break
                    fdone += 1
            drain(bg)
            drain(fg)

        NF_PIECES = (2 * NSUB + 2) + 12 + 3 * NSUB + 2
        NB_PIECES = 14 * NSUB
        drain(front(0))
        for s in range(NS):
            if s + 1 < NS:
                interleave(front(s + 1), back(s), NF_PIECES, NB_PIECES)
            else:
                drain(back(s))

    k.barrier()
    with ExitStack() as es2:
        def sb2(name, shape, dt=F32):
            return es2.enter_context(nc.sbuf_tensor(name, list(shape), dt))

        wup = sb2("wup_bf", [128, 8, 2 * DFF], BF16)
        wdown = sb2("wdown_bf", [128, NFC, D], BF16)
        g2 = sb2("g2_sb", [128, 8])
        cw = sb2("cw_sb", [128, 2 * NFC, 3])
        cb = sb2("cb_sb", [128, 2 * NFC])
        k.dma("sp", g2[:, :], g2_d[:, :], "s_g1")
        k.dma("sp", cw[:, :, :], cw_d[:, :, :], "s_gup")
        k.dma("sp", cb[:, :], cb_d[:, :], "s_negb")
        k.dma("sp", wpost[:, :], wpost2_d[:, :], "wpost")
        xa = [sb2("xa%d" % i, [128, D]) for i in range(2)]
        h2tok = [sb2("h2tok%d" % i, [128, D], BF16) for i in range(1)]
        junk2 = sb2("junk2", [128, D], BF16)
        ssq2 = sb2("ssq2", [128, 8])
        lnv2 = sb2("lnv2", [128, 8])
        rstd2 = sb2("rstd2", [128, 8])
        h2T = sb2("h2T", [128, 8, TS], BF16)
        gT = sb2("gT", [128, NFC, TS], BF16)
        tv = [sb2("tv%d" % i, [128, TS]) for i in range(2)]
        tg = [sb2("tg%d" % i, [128, TS]) for i in range(2)]
        ge = [sb2("ge%d" % i, [128, TS], BF16) for i in range(2)]
        halo = [sb2("halo%d" % i, [128, 2 * NFC, 2]) for i in range(2)]
        fix = sb2("fix", [128, 2 * NFC, 2])
        fta = sb2("fta", [128, 2 * NFC])
        ftb = sb2("ftb", [128, 2 * NFC])
        ytmp2 = sb2("ytmp2", [128, D])
        obuf = [sb2("obuf%d" % i, [128, D]) for i in range(2)]
        memset("dve", halo[0][:, :, :], 0.0)
        memset("dve", halo[1][:, :, :], 0.0)

        def p2_An(s, j):
            r0 = s * TS + j * 128
            xin = xa[j % 2]
            k.dma("sp", xin[:, :], x1_d[r0:r0 + 128, :], "xa%d" % (j % 2))
            ht = h2tok[0]
            act(junk2[:, :], xin[:, :], AF.Square, accum=ssq2[:, j:j + 1])
            rstd_from_ss(ssq2[:, j:j + 1], lnv2[:, j:j + 1], rstd2[:, j:j + 1], D)
            act(ht[:, :], xin[:, :], AF.Identity, scale=rstd2[:, j:j + 1])

        def p2_At(s, j):
            ht = h2tok[0]
            for kk in range(8):
                tr(P0bf[:, kk * 128:(kk + 1) * 128], ht[:, kk * 128:(kk + 1) * 128], ident_bf[:, :])
            cp("dve", h2T[:, :, j * 128:(j + 1) * 128],
               P0bf[:, 0:1024].rearrange("p (a t) -> p a t", a=8))

        cbank = [0]

        def p2_C(s, j):
            tsl = slice(j * 128, (j + 1) * 128)
            r0 = s * TS + j * 128
            ob = obuf[j % 2]
            k.dma("sp", ob[:, :], x1_d[r0:r0 + 128, :], "ost%d" % (j % 2))
            banks = []
            for hf in range(2):
                bank = 5 + cbank[0] % 3
                cbank[0] += 1
                banks.append(bank)
                for i in range(NFC):
                    mm(PS[bank][:, :], gT[:, i, tsl], wdown[:, i, hf * 512:(hf + 1) * 512],
                       start=(i == 0), stop=(i == NFC - 1))
                act(junk2[:, hf * 512:(hf + 1) * 512], PS[bank][:, :], AF.Square, accum=ssq2[:, 4 + hf:5 + hf])
            tt("dve", ssq2[:, 6:7], ssq2[:, 4:5], ssq2[:, 5:6], ALU.add)
            rstd_from_ss(ssq2[:, 6:7], lnv2[:, 6:7], rstd2[:, 6:7], D)
            for hf in range(2):
                stt(ytmp2[:, hf * 512:(hf + 1) * 512], PS[banks[hf]][:, :], rstd2[:, 6:7],
                    wpost[:, hf * 512:(hf + 1) * 512], ALU.mult, ALU.mult)
            tt("pool", ob[:, :], ytmp2[:, :], ob[:, :], ALU.add)
            k.dma("pool", out_d[r0:r0 + 128, :], ob[:, :], "ost%d" % (j % 2), is_output=True)

        ring = [ytmp2, obuf[0], obuf[1], xa[0], xa[1]]
        rsem = ["ring0", "ost0", "ost1", "xa0", "xa1"]
        rcnt = [0]
        ucnt = [0]
        g2b = g2[:, :].unsqueeze(2).broadcast_to([128, 8, 128])

        def load_wup_chunk(c):
            r = ucnt[0] % 5
            ucnt[0] += 1
            st = ring[r]
            k.dma("sp", st[:, :], wup_d[c, :, :], rsem[r])
            tt("pool" if ucnt[0] % 3 == 0 else "dve", wup[:, :, c * 128:(c + 1) * 128],
               st[:, :].rearrange("p (a n) -> p a n", a=8), g2b, ALU.mult)

        def load_wdown_chunk(i):
            r = rcnt[0] % 3
            rcnt[0] += 1
            st = ring[r]
            k.dma("sp", st[:, :], wdown_d[:, i, :], rsem[r])
            cp("act", wdown[:, i, :], st[:, :])

        def p2_B(s):
            hprev = halo[(s + 1) % 2]
            hnext = halo[s % 2]
            tt("dve", fta[:, :], cw[:, :, 1], hprev[:, :, 1], ALU.mult)
            tt("dve", ftb[:, :], cw[:, :, 0], hprev[:, :, 0], ALU.mult)
            tt("dve", fix[:, :, 0], fta[:, :], ftb[:, :], ALU.add)
            tt("dve", fix[:, :, 1], cw[:, :, 0], hprev[:, :, 1], ALU.mult)
            def gelu_mul(i):
                act(ge[i % 2][:, :], tg[i % 2][:, :], AF.Gelu_apprx_tanh)
                tt("pool", gT[:, i, :], ge[i % 2][:, :], tv[i % 2][:, :], ALU.mult)

            for i in range(NFC):
                pb = 1 + 2 * (i % 2)
                pend = []
                for (bank, cidx, tbuf) in ((pb, i, tv[i % 2]), (pb + 1, NFC + i, tg[i % 2])):
                    for kk in range(8):
                        mm(PS[bank][:, :], wup[:, kk, cidx * 128:(cidx + 1) * 128], h2T[:, kk, :],
                           start=(kk == 0), stop=(kk == 7))
                    act(tbuf[:, :], PS[bank][:, :], AF.Identity, bias=cb[:, cidx:cidx + 1],
                        scale=cw[:, cidx, 2:3])
                    cp("act", hnext[:, cidx, :], PS[bank][:, TS - 2:TS])
                    pend.append((tbuf, bank, cidx))
                if i > 0:
                    gelu_mul(i - 1)
                for (tbuf, bank, cidx) in pend:
                    stt(tbuf[:, 1:TS], PS[bank][:, 0:TS - 1], cw[:, cidx, 1:2], tbuf[:, 1:TS], ALU.mult, ALU.add)
                    stt(tbuf[:, 2:TS], PS[bank][:, 0:TS - 2], cw[:, cidx, 0:1], tbuf[:, 2:TS], ALU.mult, ALU.add)
                    tt("dve", tbuf[:, 0:2], tbuf[:, 0:2], fix[:, cidx, :], ALU.add)
            gelu_mul(NFC - 1)
            if s == 0:
                for i in range(NFC):
                    load_wdown_chunk(i)

        for j in range(NSUB):
            p2_An(0, j)
            p2_At(0, j)
        for i in range(NFC):
            load_wup_chunk(i)
            load_wup_chunk(NFC + i)
        for s in range(NS):
            p2_B(s)
            for j in range(NSUB):
                if s + 1 < NS:
                    p2_An(s + 1, j)
                p2_C(s, j)
                if s + 1 < NS:
                    p2_At(s + 1, j)

    k.finish()
    es.close()
    return nc, k


def _consts():
    c = np.zeros((128, 896), np.float32)
    c[:, 0:128] = np.eye(128, dtype=np.float32)
    p = np.arange(128)[:, None]
    i = np.arange(128)[None, :]
    m = (p <= i).astype(np.float32)
    c[:, 128:256] = m
    c[:, 256:384] = m
    sm = np.ones((128, 512), np.float32)
    sm[:, 0::128] = 0.0
    c[:, 384:896] = sm
    return c


def _chunk_rows(w):
    kc = w.shape[0] // 128
    return np.ascontiguousarray(w.reshape(kc, 128, w.shape[1]).transpose(1, 0, 2))


def _col(v):
    return np.ascontiguousarray(v.reshape(-1, 128).T)


def prep_shared(inp):
    f = lambda a: np.asarray(a, dtype=np.float32)
    w_in = f(inp["w_in"])[0]
    gq, gk, gv = w_in[:, 0:256], w_in[:, 256:512], w_in[:, 512:1024]
    glr, gg, sq = w_in[:, 1024:1040], w_in[:, 1040:1552], w_in[:, 1552:2064]
    sk, sv = w_in[:, 2064:2192], w_in[:, 2192:2320]
    skd = np.concatenate([sk[:, 0:64], sk[:, 0:64], sk[:, 64:128], sk[:, 64:128]], axis=1)
    w1 = np.concatenate([gq, gk, gg, glr, gv, sq, skd, sv], axis=1)
    assert w1.shape[1] == NC1
    sinks = f(inp["swa_sinks"])[0]
    sink_l = np.zeros((128, 4), np.float32)
    for cc in range(4):
        sink_l[0:64, cc] = sinks[2 * cc]
        sink_l[64:128, cc] = sinks[2 * cc + 1]
    conv_w = f(inp["conv_w"])[0]
    cw = np.ascontiguousarray(conv_w.T.reshape(2 * NFC, 128, 3).transpose(1, 0, 2))
    return {
        "cst": _consts(),
        "w1": _chunk_rows(w1),
        "g1": _col(f(inp["pre_mix_norm"])[0]),
        "gup": np.ascontiguousarray(f(inp["gla_gate_up"])[0]),
        "gbias": _col(f(inp["gla_gate_bias"])[0]),
        "gnorm": np.ascontiguousarray(f(inp["gla_out_norm"])[0].reshape(128, 1)),
        "sink": sink_l,
        "wout": _chunk_rows(f(inp["w_out"])[0]),
        "wpost1": np.ascontiguousarray(np.broadcast_to(f(inp["post_mix_norm"])[0][None, :], (128, D))),
        "g2": _col(f(inp["pre_ffn_norm"])[0]),
        "wup": np.ascontiguousarray(
            f(inp["w_up"])[0].reshape(8, 128, 2 * NFC, 128).transpose(2, 1, 0, 3).reshape(2 * NFC, 128, D)),
        "cw": cw,
        "cb": _col(f(inp["conv_b"])[0]),
        "wdown": _chunk_rows(f(inp["w_down"])[0]),
        "wpost2": np.ascontiguousarray(np.broadcast_to(f(inp["post_ffn_norm"])[0][None, :], (128, D))),
    }


def core_inputs(shared, x_b, pos_b):
    m = dict(shared)
    m["x"] = np.ascontiguousarray(np.asarray(x_b, dtype=np.float32))
    m["pos"] = np.ascontiguousarray(np.asarray(pos_b, dtype=np.int32).reshape(-1, 128).T)
    return m


_NC_CACHE = {}


def kernel(**inputs):
    x = np.asarray(inputs["x"])
    B, T, _ = x.shape
    pos = np.asarray(inputs["positions"])
    shared = prep_shared(inputs)
    if T not in _NC_CACHE:
        _NC_CACHE[T] = build_nc(T)[0]
    nc = _NC_CACHE[T]
    in_maps = [core_inputs(shared, x[b], pos[b]) for b in range(B)]
    res = run_bass_kernel_spmd(nc, in_maps, core_ids=list(range(B)))
    return np.stack([np.asarray(r["out"]) for r in res.results], axis=0).astype(np.float32)
```
